# Optimizing a Trainium2 kernel written in Bass

```python
import math
import jax, jax.numpy as jnp
from jax import lax
import numpy as np

D_MODEL = 1024
BATCH = 16
SEQ = 2048
DEPTH = 2

HEAD_DIM = 64
Q_BLOCK = 128
SCALE = HEAD_DIM ** -0.5
A_HEADS = 4
B_HEADS = 6
C_HEADS = 6
D_PAIRS = ((128, 1), (512, 4), (2048, 16))
D_GROUPS = len(D_PAIRS)
D_HEADS_PER_GROUP = 2
D_HEADS = D_GROUPS * D_HEADS_PER_GROUP
N_BRANCH = 4
FFN_HIDDEN = 2816
REL_BUCKETS = 32
REL_MAX_DIST = 128
REL_HEADS = A_HEADS + D_HEADS
RMS_EPS = 1e-6
NEG_INF = -1e30
IN_SPLITS = (
    A_HEADS * 2 * HEAD_DIM, A_HEADS * 2 * HEAD_DIM, A_HEADS * 2 * HEAD_DIM,
    B_HEADS * HEAD_DIM, B_HEADS * HEAD_DIM, B_HEADS * HEAD_DIM,
    C_HEADS * HEAD_DIM, C_HEADS * HEAD_DIM, C_HEADS * HEAD_DIM, C_HEADS,
    D_HEADS * HEAD_DIM, D_HEADS * HEAD_DIM, D_HEADS * HEAD_DIM,
    N_BRANCH * D_MODEL,
)
D_IN = sum(IN_SPLITS)
BRANCH_WIDTHS = (A_HEADS * 2 * HEAD_DIM, B_HEADS * HEAD_DIM, C_HEADS * HEAD_DIM, D_HEADS_PER_GROUP * HEAD_DIM)

kernel_name = 'hybrid_gated_parallel_mixers'


def _offsets(sizes):
    return [int(v) for v in np.cumsum(sizes)[:-1]]


def rms_norm(x, g):
    xf = x.astype(jnp.float32)
    y = xf * lax.rsqrt(jnp.mean(xf * xf, axis=-1, keepdims=True) + RMS_EPS)
    return (y * g.astype(jnp.float32)).astype(x.dtype)


def swiglu(h, w_i, w_o):
    gate, up = jnp.split(h @ w_i, 2, axis=-1)
    return (jax.nn.silu(gate) * up) @ w_o


def split_heads(t, n):
    b, s, _ = t.shape
    return t.reshape(b, s, n, -1).transpose(0, 2, 1, 3)


def merge_heads(t):
    b, h, s, d = t.shape
    return t.transpose(0, 2, 1, 3).reshape(b, s, h * d)


def rel_bucket(n):
    max_exact = REL_BUCKETS // 2
    nf = jnp.maximum(n, 1).astype(jnp.float32)
    large = max_exact + (jnp.log(nf / max_exact) / math.log(REL_MAX_DIST / max_exact)
                         * (REL_BUCKETS - max_exact)).astype(jnp.int32)
    large = jnp.minimum(large, REL_BUCKETS - 1)
    return jnp.where(n < max_exact, n, large)


def causal_rel_bias(table, q0, kv_len):
    n = (q0 + jnp.arange(Q_BLOCK))[:, None] - jnp.arange(kv_len)[None, :]
    return jnp.moveaxis(table[rel_bucket(jnp.maximum(n, 0))], -1, 0).astype(jnp.float32)


def causal_mask(q0, kv_len, strict):
    t = (q0 + jnp.arange(Q_BLOCK))[:, None]
    s = jnp.arange(kv_len)[None, :]
    return s < t if strict else s <= t


def sweep_blocks(block_fn, seq):
    return jnp.concatenate([block_fn(i * Q_BLOCK, (i + 1) * Q_BLOCK) for i in range(seq // Q_BLOCK)], axis=2)


def diff_attention(q, k, v, q_gain, k_gain, lam_vecs, subln_gain, rel_table, lam_init):
    b, s, _ = q.shape
    q = rms_norm(q.reshape(b, s, A_HEADS, 2, HEAD_DIM), q_gain).transpose(0, 3, 2, 1, 4)
    k = rms_norm(k.reshape(b, s, A_HEADS, 2, HEAD_DIM), k_gain).transpose(0, 3, 2, 1, 4)
    v = split_heads(v, A_HEADS)
    lv = lam_vecs.astype(jnp.float32)
    lam = jnp.exp(jnp.sum(lv[0] * lv[1])) - jnp.exp(jnp.sum(lv[2] * lv[3])) + lam_init

    def block(q0, kv_len):
        sc = jnp.einsum('bmhqd,bmhkd->bmhqk', q[:, :, :, q0:q0 + Q_BLOCK], k[:, :, :, :kv_len],
                        preferred_element_type=jnp.float32) * SCALE + causal_rel_bias(rel_table, q0, kv_len)
        p = jax.nn.softmax(jnp.where(causal_mask(q0, kv_len, False), sc, NEG_INF), axis=-1)
        w = p[:, 0] - lam * p[:, 1]
        return jnp.einsum('bhqk,bhkd->bhqd', w.astype(v.dtype), v[:, :, :kv_len])

    o = sweep_blocks(block, s)
    o = rms_norm(o, subln_gain) * (1.0 - lam_init)
    return merge_heads(o)


def stick_breaking_attention(q, k, v):
    b, s, _ = q.shape
    q, k, v = split_heads(q, B_HEADS), split_heads(k, B_HEADS), split_heads(v, B_HEADS)

    def block(q0, kv_len):
        z = jnp.einsum('bhqd,bhkd->bhqk', q[:, :, q0:q0 + Q_BLOCK], k[:, :, :kv_len],
                       preferred_element_type=jnp.float32) * SCALE
        mask = causal_mask(q0, kv_len, True)
        u = jnp.where(mask, jax.nn.log_sigmoid(-z), 0.0)
        tail = lax.cumsum(u, axis=3, reverse=True) - u
        a = jnp.where(mask, jnp.exp(jax.nn.log_sigmoid(z) + tail), 0.0)
        return jnp.einsum('bhqk,bhkd->bhqd', a.astype(v.dtype), v[:, :, :kv_len])

    return merge_heads(sweep_blocks(block, s))


def forgetting_attention(q, k, v, f_logit, f_bias, q_gain, k_gain):
    b, s, _ = q.shape
    q = rms_norm(q.reshape(b, s, C_HEADS, HEAD_DIM), q_gain).transpose(0, 2, 1, 3)
    k = rms_norm(k.reshape(b, s, C_HEADS, HEAD_DIM), k_gain).transpose(0, 2, 1, 3)
    v = split_heads(v, C_HEADS)
    log_f = jax.nn.log_sigmoid((f_logit + f_bias).astype(jnp.float32))
    cum = jnp.cumsum(log_f, axis=1).transpose(0, 2, 1)

    def block(q0, kv_len):
        sc = jnp.einsum('bhqd,bhkd->bhqk', q[:, :, q0:q0 + Q_BLOCK], k[:, :, :kv_len],
                        preferred_element_type=jnp.float32) * SCALE
        sc = sc + cum[:, :, q0:q0 + Q_BLOCK, None] - cum[:, :, None, :kv_len]
        p = jax.nn.softmax(jnp.where(causal_mask(q0, kv_len, False), sc, NEG_INF), axis=-1)
        return jnp.einsum('bhqk,bhkd->bhqd', p.astype(v.dtype), v[:, :, :kv_len])

    return merge_heads(sweep_blocks(block, s))


def dilated_attention(q, k, v, q_gain, k_gain, rel_table):
    b, s, _ = q.shape
    shape5 = (b, s, D_GROUPS, D_HEADS_PER_GROUP, HEAD_DIM)
    q = rms_norm(q.reshape(shape5), q_gain)
    k = rms_norm(k.reshape(shape5), k_gain)
    v = v.reshape(shape5)
    outs, lses = [], []
    for g, (window, dilation) in enumerate(D_PAIRS):
        dist = dilation * jnp.arange(window // dilation + 1)
        bias = rel_table[rel_bucket(dist)][:, g * D_HEADS_PER_GROUP:(g + 1) * D_HEADS_PER_GROUP]
        bias = bias.T.astype(jnp.float32)
        qg, kg, vg = q[:, :, g], k[:, :, g], v[:, :, g]

        def block(q0):
            pos = q0 + jnp.arange(Q_BLOCK)[:, None] - dist[None, :]
            valid = pos >= 0
            idx = jnp.maximum(pos, 0)
            kb = jnp.take(kg, idx, axis=1)
            vb = jnp.take(vg, idx, axis=1)
            qb = lax.dynamic_slice_in_dim(qg, q0, Q_BLOCK, axis=1)
            sc = jnp.einsum('bqhd,bqkhd->bhqk', qb, kb, preferred_element_type=jnp.float32) * SCALE
            sc = jnp.where(valid, sc + bias[None, :, None, :], NEG_INF)
            m = jnp.max(sc, axis=-1, keepdims=True)
            e = jnp.exp(sc - m)
            l = jnp.sum(e, axis=-1, keepdims=True)
            o = jnp.einsum('bhqk,bqkhd->bqhd', (e / l).astype(vg.dtype), vb)
            lse = (m + jnp.log(l))[..., 0].transpose(0, 2, 1)
            return o, lse

        o, lse = lax.map(block, jnp.arange(s // Q_BLOCK) * Q_BLOCK)
        outs.append(o.transpose(1, 0, 2, 3, 4).reshape(b, s, D_HEADS_PER_GROUP, HEAD_DIM))
        lses.append(lse.transpose(1, 0, 2, 3).reshape(b, s, D_HEADS_PER_GROUP))
    alpha = jax.nn.softmax(jnp.stack(lses), axis=0)
    o = jnp.sum(alpha[..., None] * jnp.stack(outs).astype(jnp.float32), axis=0)
    return o.reshape(b, s, -1).astype(v.dtype)


def setup_inputs(seed: int = 0) -> dict:
    key = jax.random.key(seed)
    ks = iter(jax.random.split(key, 32))
    L, D = DEPTH, D_MODEL

    def nrm(shape, scale):
        return jax.random.normal(next(ks), shape, jnp.float32) * scale

    def gain(shape):
        return 1.0 + nrm(shape, 0.02)

    return {
        'x': nrm((BATCH, SEQ, D), 1.0),
        'rel_table': nrm((REL_BUCKETS, REL_HEADS), 0.2),
        'ffn1_norm': gain((L, D)),
        'ffn1_w_in': nrm((L, D, 2 * FFN_HIDDEN), D ** -0.5),
        'ffn1_w_out': nrm((L, FFN_HIDDEN, D), FFN_HIDDEN ** -0.5),
        'mix_norm': gain((L, D)),
        'w_in': nrm((L, D, D_IN), D ** -0.5),
        'gate_bias': nrm((L, N_BRANCH * D), 0.02),
        'forget_bias': 2.0 + nrm((L, C_HEADS), 0.1),
        'a_q_norm': gain((L, HEAD_DIM)),
        'a_k_norm': gain((L, HEAD_DIM)),
        'a_lambda': nrm((L, 4, HEAD_DIM), 0.1),
        'a_subln': gain((L, 2 * HEAD_DIM)),
        'c_q_norm': gain((L, HEAD_DIM)),
        'c_k_norm': gain((L, HEAD_DIM)),
        'd_q_norm': gain((L, HEAD_DIM)),
        'd_k_norm': gain((L, HEAD_DIM)),
        'w_branch': jnp.concatenate([nrm((L, w, D), w ** -0.5) for w in BRANCH_WIDTHS], axis=1),
        'w_out': nrm((L, D, D), D ** -0.5),
        'ffn2_norm': gain((L, D)),
        'ffn2_w_in': nrm((L, D, 2 * FFN_HIDDEN), D ** -0.5),
        'ffn2_w_out': nrm((L, FFN_HIDDEN, D), FFN_HIDDEN ** -0.5),
    }


def reference(x, rel_table, ffn1_norm, ffn1_w_in, ffn1_w_out, mix_norm, w_in, gate_bias, forget_bias,
              a_q_norm, a_k_norm, a_lambda, a_subln, c_q_norm, c_k_norm, d_q_norm, d_k_norm,
              w_branch, w_out, ffn2_norm, ffn2_w_in, ffn2_w_out):
    b, s, _ = x.shape
    in_idx = _offsets(IN_SPLITS)
    br_idx = _offsets(BRANCH_WIDTHS)
    for l in range(DEPTH):
        x = x + 0.5 * swiglu(rms_norm(x, ffn1_norm[l]), ffn1_w_in[l], ffn1_w_out[l])
        h = rms_norm(x, mix_norm[l])
        (aq, ak, av, bq, bk, bv, cq, ck, cv, cf, dq, dk, dv, gl) = jnp.split(h @ w_in[l], in_idx, axis=-1)
        lam_init = 0.8 - 0.6 * math.exp(-0.3 * l)
        oa = diff_attention(aq, ak, av, a_q_norm[l], a_k_norm[l], a_lambda[l], a_subln[l],
                            rel_table[:, :A_HEADS], lam_init)
        ob = stick_breaking_attention(bq, bk, bv)
        oc = forgetting_attention(cq, ck, cv, cf, forget_bias[l], c_q_norm[l], c_k_norm[l])
        od = dilated_attention(dq, dk, dv, d_q_norm[l], d_k_norm[l], rel_table[:, A_HEADS:])
        wa, wb, wc, wd = jnp.split(w_branch[l], br_idx, axis=0)
        gates = jax.nn.sigmoid(gl + gate_bias[l]).reshape(b, s, N_BRANCH, D_MODEL)
        merged = (gates[:, :, 0] * (oa @ wa) + gates[:, :, 1] * (ob @ wb)
                  + gates[:, :, 2] * (oc @ wc) + gates[:, :, 3] * (od @ wd))
        x = x + merged @ w_out[l]
        x = x + 0.5 * swiglu(rms_norm(x, ffn2_norm[l]), ffn2_w_in[l], ffn2_w_out[l])
    return x
```

```python
import numpy as np
import math
import concourse.bass as bass
import concourse.mybir as mybir
from concourse.bass_utils import run_bass_kernel_spmd
from contextlib import ExitStack

F32 = mybir.dt.float32
BF16 = mybir.dt.bfloat16
AF = mybir.ActivationFunctionType
ALU = mybir.AluOpType
AX = mybir.AxisListType

COMPUTE = ("pe", "act", "dve", "pool")
ENGS = ("pe", "act", "dve", "pool", "sp")

P = 128
SEQ = 2048
NT = 16
DM = 1024
KC = 8
FH = 2816
FC = 22
DEPTH = 2
NEG = -30000.0
EPS = 1e-6
SCALE = 0.125

OFF_AQ, OFF_AK, OFF_AV = 0, 512, 1024
OFF_BQ, OFF_BK, OFF_BV = 1536, 1920, 2304
OFF_CQ, OFF_CK, OFF_CV, OFF_CF = 2688, 3072, 3456, 3840
OFF_DQ, OFF_DK, OFF_DV = 3846, 4230, 4614
OFF_GL = 4998

HPUS = [("B", 0), ("B", 1), ("B", 2), ("C", 0), ("A", 0), ("C", 1), ("A", 1), ("C", 2), ("A", 2), ("A", 3),
        ("D", 0), ("D", 1), ("D", 2)]


def hpu_cols(kind, n):
    if kind == "A":
        return OFF_AQ + n * 128, OFF_AK + n * 128, OFF_AV + n * 128
    if kind == "B":
        return OFF_BQ + n * 128, OFF_BK + n * 128, OFF_BV + n * 128
    if kind == "C":
        return OFF_CQ + n * 128, OFF_CK + n * 128, OFF_CV + n * 128
    return OFF_DQ + n * 128, OFF_DK + n * 128, OFF_DV + n * 128


def ot_chunk(kind, n):
    return {"B": 0, "C": 3, "A": 6, "D": 10}[kind] + (0 if kind == "D" else n)


def cid_ffn(j, i, gu):
    return j * 44 + i * 2 + gu


def cid_ffo(j, r):
    return 88 + j * 22 + r


def cid_inp(n, part):
    return 132 + n * 3 + part


def cid_gate(dc, i):
    return 171 + dc * 4 + i


def cid_wbr(dc, part):
    return 203 + dc * 2 + part


def cid_wout(half, q):
    return 219 + half * 4 + q


NCH = 227

PV_FB = 0
PV_AQ = 6
PV_AK = 70
PV_LAM = 134
PV_SUB = 390
PV_CQ = 518
PV_CK = 582
PV_DQ = 646
PV_DK = 710
PV_REL31 = 774
NPV = 784

D_TYPES = [(0, 0), (0, 1), (1, 0), (1, 1), (1, "far"), (1, 4), (2, 0), (2, 1), (2, "far")]
D_PAIRS = ((128, 1), (512, 4), (2048, 16))
M_STRICT = 9
M_ZERO = 10
NMASK = 11


def rel_bucket_np(n):
    n = np.asarray(n)
    nf = np.maximum(n, 1).astype(np.float32)
    large = 16 + (np.log(nf / np.float32(16)) / np.float32(math.log(128 / 16)) * np.float32(16)).astype(np.int32)
    large = np.minimum(large, 31)
    return np.where(n < 16, n, large)


def d_type_index(g, delta):
    if g == 0:
        return {0: 0, 1: 1}[delta]
    if g == 1:
        return {0: 2, 1: 3, 2: 4, 3: 4, 4: 5}[delta]
    return 6 if delta == 0 else (7 if delta == 1 else 8)


def d_type_dist(g, ty):
    return {0: 0, 1: 1, "far": 2, 4: 4}[ty]


class Buf:
    __slots__ = ("name", "w", "r", "excl")

    def __init__(self, name, excl=False):
        self.name = name
        self.w = {}
        self.r = {}
        self.excl = excl


class Op:
    __slots__ = ("fn", "waits", "inc", "dq")

    def __init__(self, fn, waits, dq):
        self.fn = fn
        self.waits = waits
        self.inc = False
        self.dq = dq


class Sched:
    def __init__(self, nc, stack):
        self.nc = nc
        self.stack = stack
        self.ops = {e: [] for e in ENGS}
        self.known = {e: {} for e in ENGS}
        self.sems = {}
        for e in COMPUTE:
            self.sems[e] = stack.enter_context(nc.semaphore("s_" + e))
        self.dma_cnt = {}
        self.clocks = {}

    def issue(self, eng, fn, reads=(), writes=(), dq=None):
        deps = {}
        for b in reads:
            for q, i in b.w.items():
                if deps.get(q, -1) < i:
                    deps[q] = i
            if b.excl:
                for q, i in b.r.items():
                    if q != eng and deps.get(q, -1) < i:
                        deps[q] = i
        for b in writes:
            for q, i in b.w.items():
                if eng == "pe" and q == "pe":
                    continue
                if deps.get(q, -1) < i:
                    deps[q] = i
            for q, i in b.r.items():
                if deps.get(q, -1) < i:
                    deps[q] = i
        kn = self.known[eng]
        waits = []
        for q, i in deps.items():
            if kn.get(q, -1) >= i:
                continue
            waits.append((q, i))
            for q2, i2 in self.clocks[(q, i)].items():
                if kn.get(q2, -1) < i2:
                    kn[q2] = i2
            kn[q] = i
            if q in COMPUTE:
                self.ops[q][i].inc = True
        op = Op(fn, waits, dq)
        idx = len(self.ops[eng])
        self.ops[eng].append(op)
        if dq is None:
            tok = (eng, idx)
        else:
            if dq not in self.sems:
                self.sems[dq] = self.stack.enter_context(self.nc.semaphore("d_" + dq))
                self.dma_cnt[dq] = 0
            self.dma_cnt[dq] += 1
            tok = (dq, self.dma_cnt[dq])
        self.clocks[tok] = dict(kn)
        for b in reads:
            if b.r.get(tok[0], -1) < tok[1]:
                b.r[tok[0]] = tok[1]
        for b in writes:
            b.w = {tok[0]: tok[1]}
            b.r = {}
        return tok

    def fence_group(self, bufs, dq):
        for b in bufs:
            b.w = {dq: self.dma_cnt[dq]}

    def alias(self, new_bufs, old_bufs):
        acc = {}
        for o in old_bufs:
            for d in (o.w, o.r):
                for q, i in d.items():
                    if acc.get(q, -1) < i:
                        acc[q] = i
        for n in new_bufs:
            for q, i in acc.items():
                if n.r.get(q, -1) < i:
                    n.r[q] = i

    def emit(self, final_dqs=()):
        nc = self.nc
        semval = {}
        for e in COMPUTE:
            c = 0
            vals = []
            for op in self.ops[e]:
                if op.inc:
                    c += 1
                vals.append(c)
            semval[e] = vals

        def run(engname, e):
            for op in self.ops[engname]:
                for (q, i) in op.waits:
                    if q in COMPUTE:
                        e.wait_ge(self.sems[q], semval[q][i])
                    else:
                        e.wait_ge(self.sems[q], 16 * i)
                ins = op.fn(e)
                if op.dq is not None:
                    ins.then_inc(self.sems[op.dq], 16)
                elif op.inc:
                    ins.then_inc(self.sems[engname], 1)
            if engname == "sp":
                for dq in final_dqs:
                    e.wait_ge(self.sems[dq], 16 * self.dma_cnt[dq])

        with nc.Block() as block:
            @block.tensor
            def _(e):
                run("pe", e)

            @block.scalar
            def _(e):
                run("act", e)

            @block.vector
            def _(e):
                run("dve", e)

            @block.gpsimd
            def _(e):
                run("pool", e)

            @block.sync
            def _(e):
                run("sp", e)


class Rot:
    def __init__(self, items):
        self.items = items
        self.k = 0

    def next(self):
        it = self.items[self.k % len(self.items)]
        self.k += 1
        return it


def build_program(nb=2, layers=2, stop=None, debug=False):
    nc = bass.Bass("TRN2", target_bir_lowering=False)
    x_d = nc.dram_tensor("x", [nb, SEQ, DM], F32, kind="ExternalInput").ap()
    ws_d = nc.dram_tensor("wstream", [DEPTH, NCH, P, 1024], F32, kind="ExternalInput").ap()
    gT_d = nc.dram_tensor("gT", [DEPTH, 3, P, KC], F32, kind="ExternalInput").ap()
    gb_d = nc.dram_tensor("gbias", [DEPTH, P, 32], F32, kind="ExternalInput").ap()
    pv_d = nc.dram_tensor("pvec", [1, DEPTH * NPV], F32, kind="ExternalInput").ap()
    wcf_d = nc.dram_tensor("wcf", [DEPTH, P, KC, 6], F32, kind="ExternalInput").ap()
    biasA_d = nc.dram_tensor("biasA", [P, 8, P], F32, kind="ExternalInput").ap()
    biasD_d = nc.dram_tensor("biasD", [P, 18, P], F32, kind="ExternalInput").ap()
    masks_d = nc.dram_tensor("masks", [P, NMASK, P], F32, kind="ExternalInput").ap()
    csel_d = nc.dram_tensor("csel", [P, 8], F32, kind="ExternalInput").ap()
    out_d = nc.dram_tensor("out", [nb, SEQ, DM], F32, kind="ExternalOutput").ap()
    dbg_d = nc.dram_tensor("dbg_oT", [P, 11264], F32, kind="ExternalOutput").ap() if debug else None

    with ExitStack() as st:
        S = Sched(nc, st)

        def sb(name, shape, dt):
            return st.enter_context(nc.sbuf_tensor("sb_" + name, shape, dt))

        def psum(name, shape, dt):
            return st.enter_context(nc.psum_tensor(name, shape, dt))

        PE = lambda fn, r=(), w=(): S.issue("pe", fn, r, w)
        ACT = lambda fn, r=(), w=(): S.issue("act", fn, r, w)
        DVE = lambda fn, r=(), w=(): S.issue("dve", fn, r, w)
        POOL = lambda fn, r=(), w=(): S.issue("pool", fn, r, w)
        DMA = lambda fn, r=(), w=(), dq=None: S.issue("sp", fn, r, w, dq)

        def MM(out, lhsT, rhs, start, stop, r, w, skip=False):
            PE(lambda e: e.matmul(out, lhsT=lhsT, rhs=rhs, start=start, stop=stop, skip_group_check=skip), r, w)

        def TRN(out, in_, r, w):
            PE(lambda e: e.transpose(out=out, in_=in_, identity=ident[:]), r, w)

        def AV(out, in_, func, r, w, scale=None, bias=None, accum=None):
            kw = {}
            if scale is not None:
                kw["scale"] = scale
            if bias is not None:
                kw["bias"] = bias
            if accum is not None:
                kw["accum_out"] = accum
            ACT(lambda e: e.activation(out=out, in_=in_, func=func, **kw), r, w)

        def TT(eng, out, in0, in1, op, r, w):
            S.issue(eng, lambda e: e.tensor_tensor(out=out, in0=in0, in1=in1, op=op), r, w)

        def TS(out, in0, s1, s2, op0, op1, r, w):
            if op1 is None:
                DVE(lambda e: e.tensor_scalar(out=out, in0=in0, scalar1=s1, scalar2=None, op0=op0), r, w)
            else:
                DVE(lambda e: e.tensor_scalar(out=out, in0=in0, scalar1=s1, scalar2=s2, op0=op0, op1=op1), r, w)

        def STT(out, in0, scalar, in1, op0, op1, r, w):
            DVE(lambda e: e.scalar_tensor_tensor(out=out, in0=in0, scalar=scalar, in1=in1, op0=op0, op1=op1), r, w)

        def CP(eng, out, in_, r, w):
            S.issue(eng, lambda e: e.tensor_copy(out=out, in_=in_), r, w)

        def RCP(out, in_, r, w):
            DVE(lambda e: e.reciprocal(out=out, in_=in_), r, w)

        def MSET(eng, ap, val, w):
            S.issue(eng, lambda e: e.memset(ap, val), (), w)

        def DMAX_(out, in_, r, w, dq):
            DMA(lambda e: e.dma_start(out=out, in_=in_), r, w, dq)

        XR = sb("XR", [P, NT, DM], F32)
        bX = [Buf(f"X{t}") for t in range(NT)]
        hT = sb("hT", [P, KC, SEQ], BF16)
        bhT = [Buf(f"hT{t}") for t in range(NT)]
        AR = sb("AR", [P, 30720], BF16)
        NSTG = 4
        STG = sb("STG", [P, NSTG, 1024], F32)
        bSTG = [Buf(f"stg{k}") for k in range(NSTG)]
        WB = sb("WB", [P, 4, 2048], BF16)
        bWBh = [[Buf(f"wb{k}a"), Buf(f"wb{k}b")] for k in range(4)]
        bWB = None
        ident = sb("ident", [P, P], BF16)
        identf = sb("identf", [P, P], F32)
        b_ident = Buf("ident")
        gT = sb("gT", [P, DEPTH * 3, KC], F32)
        b_gT = Buf("gT")
        junk_v = {"ffn": hT[:, 7, 1024:2048], "mix": AR[:, 10272:11296]}
        xn_v = {"ffn": [hT[:, 5, 1024:2048], hT[:, 6, 1024:2048]],
                "mix": [AR[:, 8224:9248], AR[:, 9248:10272]]}
        junk_e = AR[:, 26656:26784]
        bxn = [Buf("xn0"), Buf("xn1")]
        ssr = sb("ssr", [P, 2], F32)
        bss = [Buf("ss0"), Buf("ss1")]
        ssrot = Rot([(ssr[:, k:k + 1], bss[k]) for k in range(2)])
        xnk = [0]

        PB = [psum(f"pb{k}", [P, 512], F32) for k in range(8)]
        bPB = [Buf(f"pb{k}", excl=True) for k in range(8)]
        TRB = PB[7].bitcast(BF16)
        TRB6 = PB[6].bitcast(BF16)
        bTR = [bPB[7]] * 8

        DMA(lambda e: e.dma_start(out=gT[:], in_=gT_d.rearrange("l j p k -> p (l j) k")), w=[b_gT], dq="cst")
        POOL(lambda e: e.memset(identf[:], 1.0), w=[b_ident])
        POOL(lambda e: e.affine_select(out=identf[:], in_=identf[:], pattern=[[1, P]], compare_op=ALU.is_equal,
                                       fill=0.0, base=0, channel_multiplier=-1), r=[b_ident], w=[b_ident])
        POOL(lambda e: e.tensor_copy(out=ident[:], in_=identf[:]), r=[b_ident], w=[b_ident])

        stg_k = [0]

        def load_chunk(l, cid):
            k = stg_k[0] % NSTG
            stg_k[0] += 1
            DMA(lambda e: e.dma_start(out=STG[:, k, :], in_=ws_d[l, cid]), w=[bSTG[k]], dq=f"stg{k}")
            return k

        def cast_chunk(k, out_ap, in_view, wbufs, eng="pool"):
            src = in_view(STG[:, k, :])
            if eng == "act":
                ACT(lambda e: e.copy(out=out_ap, in_=src), r=[bSTG[k]], w=wbufs)
            else:
                POOL(lambda e: e.tensor_copy(out=out_ap, in_=src), r=[bSTG[k]], w=wbufs)

        def make_hT(t, col0, gidx, phase):
            ss, bs = ssrot.next()
            xt, bx = xn_v[phase][xnk[0] % 2], bxn[xnk[0] % 2]
            xnk[0] += 1
            ACT(lambda e: e.activation(out=xt, in_=XR[:, t, :], func=AF.Square, accum_out=ss), r=[bX[t]], w=[bs, bx])
            ACT(lambda e: e.activation(out=ss, in_=ss, func=AF.Ln, scale=1.0 / DM, bias=EPS), r=[bs], w=[bs])
            ACT(lambda e: e.activation(out=ss, in_=ss, func=AF.Exp, scale=-0.5), r=[bs], w=[bs])
            DVE(lambda e: e.tensor_scalar(out=xt, in0=XR[:, t, :], scalar1=ss, scalar2=None, op0=ALU.mult),
                r=[bX[t], bs], w=[bx])
            for kc in range(KC):
                PE(lambda e, kc=kc: e.transpose(out=TRB[:, kc * P:(kc + 1) * P], in_=xt[:, kc * P:(kc + 1) * P],
                                                identity=ident[:]), r=[bx, b_ident], w=[bTR[kc]])
            DVE(lambda e: e.tensor_tensor(out=hT[:, :, col0:col0 + P],
                                          in0=TRB.rearrange("p (k c) -> p k c", k=KC),
                                          in1=gT[:, gidx, :].unsqueeze(2).to_broadcast([P, KC, P]), op=ALU.mult),
                r=[bPB[7], b_gT], w=[bhT[col0 // P]])

        actT = AR[:, 0:22528].rearrange("p (i t) -> p i t", i=FC)
        bact = [[Buf(f"act{i}_{c}") for c in range(2)] for i in range(FC)]
        WoG = [AR[:, 22528 + g * 4096: 22528 + (g + 1) * 4096].rearrange("p (r c) -> p r c", r=4) for g in range(2)]
        bWoG = [Buf("wog0"), Buf("wog1")]
        bsg = [Buf("sg0"), Buf("sg1")]
        sgrot = Rot([(hT[:, 3 + k, 1024:2048].bitcast(F32), bsg[k]) for k in range(2)])
        wbrot = [0]
        pbrot = [0]
        GROUPS = [(r, min(r + 4, FC)) for r in range(0, FC, 4)]

        def ffn(l, j):
            S.alias(ffn_ar, ar_all)
            S.alias(bxn + bsg, bhT[8:16])
            gidx = l * 3 + (0 if j == 0 else 2)
            NG = len(GROUPS)

            def prep_in(i):
                s_ = i % 4
                for gu in range(2):
                    k = load_chunk(l, cid_ffn(j, i, gu))
                    cast_chunk(k, WB[:, s_, gu * 1024:(gu + 1) * 1024], lambda v: v, [bWBh[s_][gu]],
                               eng=("act" if gu == 0 else "pool"))

            def prep_out(g):
                r0, r1 = GROUPS[g]
                for r in range(r0, r1):
                    k = load_chunk(l, cid_ffo(j, r))
                    cast_chunk(k, WoG[g % 2][:, r - r0, :], lambda v: v, [bWoG[g % 2]],
                               eng=("act" if r % 2 == 0 else "pool"))

            for ps in range(2):
                prep_in(0)
                prep_in(1)
                for tt in range(8):
                    make_hT(ps * 8 + tt, tt * P, gidx, "ffn")
                for i in range(FC):
                    if i + 2 < FC:
                        prep_in(i + 2)
                    if i == FC - 3:
                        prep_out(0)
                    s_ = i % 4
                    for c in range(2):
                        banks = []
                        for gu in range(2):
                            b = pbrot[0] % 4
                            pbrot[0] += 1
                            banks.append(b)
                            for kc in range(KC):
                                MM(PB[b][:], WB[:, s_, gu * 1024 + kc * P: gu * 1024 + (kc + 1) * P],
                                   hT[:, kc, c * 512:(c + 1) * 512], (kc == 0), (kc == KC - 1),
                                   [bWBh[s_][gu]] + bhT[c * 4:(c + 1) * 4], [bPB[b]])
                        sg, bs_ = sgrot.next()
                        AV(sg, PB[banks[0]][:], AF.Silu, [bPB[banks[0]]], [bs_])
                        TT("dve", actT[:, i, c * 512:(c + 1) * 512], sg, PB[banks[1]][:], ALU.mult,
                           [bs_, bPB[banks[1]]], [bact[i][c]])
                for g, (r0, r1) in enumerate(GROUPS):
                    if g + 1 < NG:
                        prep_out(g + 1)
                    wg = WoG[g % 2]
                    bwg = bWoG[g % 2]
                    for tt in range(8):
                        t = ps * 8 + tt
                        for dh in range(2):
                            b = 4 + (pbrot[0] % 2)
                            pbrot[0] += 1
                            for r in range(r0, r1):
                                MM(PB[b][:], actT[:, r, tt * P:(tt + 1) * P], wg[:, r - r0, dh * 512:(dh + 1) * 512],
                                   (r == r0), (r == r1 - 1), [bact[r][tt // 4], bwg], [bPB[b]])
                            STT(XR[:, t, dh * 512:(dh + 1) * 512], PB[b][:], 0.5, XR[:, t, dh * 512:(dh + 1) * 512],
                                ALU.mult, ALU.add, [bPB[b], bX[t]], [bX[t]])

        xsc_d = nc.dram_tensor("xscratch", [NT, P, DM], F32, kind="Internal").ap()
        XRb = XR[:].rearrange("p t d -> p (t d)").bitcast(BF16)
        XRf = XR[:].rearrange("p t d -> p (t d)")
        oT = XRb[:, 0:22528].rearrange("p (c t) -> p c t", c=11)
        boT = [[Buf(f"oT{c}_{t}") for t in range(NT)] for c in range(11)]
        mgT = [XRb[:, 22528:30720].rearrange("p (c t) -> p c t", c=8),
               AR[:, 8224:16416].rearrange("p (c t) -> p c t", c=8)]
        bmgc = [[Buf(f"mg{h}_{tc}") for tc in range(2)] for h in range(2)]
        macc = XR[:, 15, :]
        bmacc = Buf("macc")

        QTp_s = [[AR[:, 0:2048], AR[:, 28672:30720]], [XRb[:, 22528:24576], XRb[:, 24576:26624]]]
        KT_s = [AR[:, 2048:4096], XRb[:, 26624:28672]]
        VE_s = [AR[:, 4096:6176].rearrange("p (t w) -> p t w", t=NT),
                XRb[:, 28672:30752].rearrange("p (t w) -> p t w", t=NT)]
        bQK_s = [[Buf(f"qk{k}_{t}") for t in range(NT)] for k in range(2)]
        bVE_s = [[Buf(f"ve{k}_{t}") for t in range(NT)] for k in range(2)]
        bQK = bQK_s[0]
        bVE = bVE_s[0]
        set1_bufs = bQK_s[1] + bVE_s[1]
        PTb = [Buf(f"pt{k}") for k in range(4)]
        ptrot = Rot([(AR[:, 6176 + k * 512: 6176 + (k + 1) * 512], PTb[k]) for k in range(4)])
        bE32 = [Buf("e32_0"), Buf("e32_1")]
        e32rot = Rot([(AR[:, 8224 + k * 1024: 8224 + (k + 1) * 1024].bitcast(F32), bE32[k]) for k in range(2)])
        bLb = [Buf("lb0"), Buf("lb1")]
        lbrot = Rot([(AR[:, 10272 + k * 512: 10272 + (k + 1) * 512], bLb[k]) for k in range(2)])
        Ls32 = [AR[:, 11296 + m * 1024: 11296 + (m + 1) * 1024].bitcast(F32) for m in range(2)]
        bLs32 = [Buf("ls32_0"), Buf("ls32_1")]
        Lsb = [[AR[:, 13344 + (m * 2 + pp) * 512: 13344 + (m * 2 + pp + 1) * 512] for pp in range(2)] for m in range(2)]
        bLsb = [[Buf(f"lsb{m}{pp}") for pp in range(2)] for m in range(2)]
        QE = [AR[:, 8224 + m * 2048: 8224 + (m + 1) * 2048] for m in range(2)]
        KE = [AR[:, 12320 + m * 2048: 12320 + (m + 1) * 2048] for m in range(2)]
        bQE = [[Buf(f"qe{m}{c}") for c in range(4)] for m in range(2)]
        bKE = [[Buf(f"ke{m}{c}") for c in range(4)] for m in range(2)]
        accD = AR[:, 8224:8224 + 4160].bitcast(F32).rearrange("p (t w) -> p t w", t=NT)
        baccD = [Buf(f"accD{t}") for t in range(NT)]
        ex_B = bE32 + bLb + bLs32 + bLsb[0] + bLsb[1]
        ex_C = bQE[0] + bQE[1] + bKE[0] + bKE[1]
        ex_all = ex_B + ex_C + baccD + bmgc[1] + bxn
        ce32 = AR[:, 16416:17440].bitcast(F32)
        bce32 = Buf("ce32")
        cn32 = [AR[:, 17440 + k * 1024: 17440 + (k + 1) * 1024].bitcast(F32) for k in range(2)]
        bcn32 = [Buf("cn0"), Buf("cn1")]
        chi = AR[:, 19488:20000]
        clo = AR[:, 20000:20512]
        bchi = Buf("chi")
        bclo = Buf("clo")
        wrep = AR[:, 20512:21536].rearrange("p (k c) -> p k c", k=KC)
        bwrep = Buf("wrep")
        bqkv = [Buf("qkv0"), Buf("qkv1")]
        qkvrot = Rot([(AR[:, 21536 + k * 768: 21536 + (k + 1) * 768].bitcast(F32), bqkv[k]) for k in range(2)])
        sqt = AR[:, 23072:23584].bitcast(F32)
        bsq = Buf("sq")
        bqn = [Buf(f"qn{k}") for k in range(4)]
        qnrot = Rot([(AR[:, 26784 + k * 256: 26784 + (k + 1) * 256], bqn[k]) for k in range(4)])
        A0 = AR[:, 24096:25120].bitcast(F32).rearrange("p (i d) -> p i d", i=4)
        bA0 = [Buf(f"A0_{i}") for i in range(4)]
        bos = [Buf("os0"), Buf("os1")]
        osrot = Rot([(AR[:, 25120 + k * 256: 25120 + (k + 1) * 256].bitcast(F32), bos[k]) for k in range(2)])
        bob4 = [Buf("ob0"), Buf("ob1")]
        obfrot = Rot([(AR[:, 25632 + k * 512: 25632 + (k + 1) * 512].rearrange("p (i d) -> p i d", i=4), bob4[k]) for k in range(2)])
        bgsb = [Buf("gsb0"), Buf("gsb1")]
        gsbrot = Rot([(AR[:, k * 512:(k + 1) * 512], bgsb[k]) for k in range(2)])
        btmpm = [Buf("tmpm0"), Buf("tmpm1")]
        tmprot = Rot([(AR[:, 1024 + k * 1024: 1024 + (k + 1) * 1024].bitcast(F32), btmpm[k]) for k in range(2)])
        bwbd = [Buf("wbd0"), Buf("wbd1")]
        wbdrot = Rot([(AR[:, 3072 + k * 1408: 3072 + (k + 1) * 1408].rearrange("p (f c) -> p f c", f=11), bwbd[k]) for k in range(2)])
        lo_attn = bQK + bVE + PTb
        lo_merge = bgsb + btmpm + bwbd
        lo_all = lo_attn + lo_merge
        mixer_ar = (lo_all + ex_all + [bce32, bchi, bclo, bwrep, bsq] + bcn32 + bqkv + bqn + bA0 + bos + bob4)
        ffn_ar = [b for row in bact for b in row] + bWoG
        ar_all = mixer_ar + ffn_ar + bxn

        PVT = sb("pvt", [P, NPV], F32)
        b_pvt = Buf("pvt")
        GBT = sb("gbt", [P, DEPTH, 32], F32)
        cselt = sb("cselt", [P, 8], F32)
        tilesA = sb("tilesA", [P, 8, P], BF16)
        tilesD = sb("tilesD", [P, 18, P], BF16)
        maskB = sb("maskB", [P, P], BF16)
        maskC = sb("maskC", [P, P], BF16)
        negTri = sb("negTri", [P, P], BF16)
        negOnes = sb("negOnes", [P, P], BF16)
        sBt = sb("sBt", [P, 256], F32)
        ones1 = sb("ones1", [P, 1], F32)
        b_cst = Buf("cst")
        gA = sb("gA", [P, 256], F32)
        gC = sb("gC", [P, 256], F32)
        gD = sb("gD", [P, 256], F32)
        negfb = sb("negfb", [P, 6], F32)
        lamp = sb("lamp", [P, 2, 64], F32)
        lam2 = sb("lam2", [P, 2], F32)
        neglam = sb("neglam", [P, 1], F32)
        gsub = sb("gsub", [P, P], F32)
        wcf32 = sb("wcf32", [P, KC, 6], F32)
        wcfb = sb("wcfb", [P, KC, 6], BF16)
        b_lay = Buf("lay")
        ss4t = sb("ss4t", [P, 2, 4], F32)
        bss4 = [Buf("ss4_0"), Buf("ss4_1")]
        ss4rot = Rot([(ss4t[:, k, :], bss4[k]) for k in range(2)])
        rlt = sb("rlt", [P, 4], F32)
        brl = [Buf(f"rl{k}") for k in range(4)]
        rlrot = Rot([(rlt[:, k:k + 1], brl[k]) for k in range(4)])
        ss1t = sb("ss1t", [P, 2], F32)
        bss1 = [Buf("ss1_0"), Buf("ss1_1")]
        ss1rot = Rot([(ss1t[:, k:k + 1], bss1[k]) for k in range(2)])

        stA = XR[:, 0, :].rearrange("p (a c) -> p a c", a=8)
        stD = XRf[:, 1024:1024 + 2304].rearrange("p (a c) -> p a c", a=18)
        stM = XRf[:, 4096:4096 + NMASK * P].rearrange("p (a c) -> p a c", a=NMASK)
        DMA(lambda e: e.dma_start(out=XR[:, 0, :], in_=biasA_d.rearrange("p a c -> p (a c)")), w=[bX[0]], dq="x0")
        DMA(lambda e: e.dma_start(out=XRf[:, 1024:1024 + 2304], in_=biasD_d.rearrange("p a c -> p (a c)")),
            w=[bX[1], bX[2], bX[3]], dq="x1")
        DMA(lambda e: e.dma_start(out=XRf[:, 4096:4096 + NMASK * P], in_=masks_d.rearrange("p a c -> p (a c)")),
            w=[bX[4], bX[5]], dq="x4")
        DMA(lambda e: e.dma_start(out=PVT[:], in_=pv_d[:, 0:NPV].partition_broadcast(P)), w=[b_pvt], dq="pvt")
        DMA(lambda e: e.dma_start(out=GBT[:], in_=gb_d.rearrange("l p c -> p l c")), w=[b_cst], dq="cst")
        DMA(lambda e: e.dma_start(out=cselt[:], in_=csel_d), w=[b_cst], dq="cst")
        for h in range(4):
            for dl in range(2):
                DVE(lambda e, h=h, dl=dl: e.scalar_tensor_tensor(
                    out=tilesA[:, h * 2 + dl, :], in0=stA[:, h * 2 + dl, :], scalar=PVT[:, PV_REL31 + h:PV_REL31 + h + 1],
                    in1=stM[:, (0 if dl == 0 else M_ZERO), :], op0=ALU.subtract, op1=ALU.add),
                    r=[bX[0], bX[4], bX[5], b_cst, b_pvt], w=[b_cst])
        for ti in range(9):
            for hh in range(2):
                DVE(lambda e, ti=ti, hh=hh: e.tensor_tensor(out=tilesD[:, ti * 2 + hh, :], in0=stD[:, ti * 2 + hh, :],
                                                            in1=stM[:, ti, :], op=ALU.add),
                    r=[bX[1], bX[2], bX[3], bX[4], bX[5]], w=[b_cst])
        DVE(lambda e: e.tensor_copy(out=maskB[:], in_=stM[:, M_STRICT, :]), r=[bX[4], bX[5]], w=[b_cst])
        DVE(lambda e: e.tensor_copy(out=maskC[:], in_=stM[:, 0, :]), r=[bX[4], bX[5]], w=[b_cst])
        POOL(lambda e: e.memset(negOnes[:], -1.0), w=[b_cst])
        POOL(lambda e: e.memset(identf[:], -1.0), r=[b_ident], w=[b_ident])
        POOL(lambda e: e.affine_select(out=identf[:], in_=identf[:], pattern=[[-1, P]], compare_op=ALU.is_ge,
                                       fill=0.0, base=0, channel_multiplier=1), r=[b_ident], w=[b_ident])
        POOL(lambda e: e.tensor_copy(out=negTri[:], in_=identf[:]), r=[b_ident], w=[b_cst])
        POOL(lambda e: e.memset(sBt[:, 0:128], SCALE), w=[b_cst])
        POOL(lambda e: e.memset(sBt[:, 128:256], 1.0), w=[b_cst])
        POOL(lambda e: e.memset(ones1[:], 1.0), w=[b_cst])

        def layer_consts(l):
            o = 0
            DMA(lambda e: e.dma_start(out=PVT[:], in_=pv_d[:, l * NPV:(l + 1) * NPV].partition_broadcast(P)), w=[b_pvt], dq="pvt")
            lam_init = 0.8 - 0.6 * math.exp(-0.3 * l)

            def gains(dst, qo, ko):
                DVE(lambda e: e.tensor_scalar(out=dst[:, 0:128].rearrange("p (a d) -> p a d", a=2),
                                              in0=PVT[:, o + qo:o + qo + 64].unsqueeze(1).to_broadcast([P, 2, 64]),
                                              scalar1=SCALE, scalar2=None, op0=ALU.mult), r=[b_pvt], w=[b_lay])
                DVE(lambda e: e.tensor_copy(out=dst[:, 128:256].rearrange("p (a d) -> p a d", a=2),
                                            in_=PVT[:, o + ko:o + ko + 64].unsqueeze(1).to_broadcast([P, 2, 64])),
                    r=[b_pvt], w=[b_lay])
            gains(gA, PV_AQ, PV_AK)
            gains(gC, PV_CQ, PV_CK)
            gains(gD, PV_DQ, PV_DK)
            DVE(lambda e: e.tensor_scalar(out=negfb[:], in0=PVT[:, o + PV_FB:o + PV_FB + 6], scalar1=-1.0, scalar2=None,
                                          op0=ALU.mult), r=[b_pvt], w=[b_lay])
            lvr = PVT[:, o + PV_LAM:o + PV_LAM + 256].rearrange("p (a b d) -> p a b d", a=2, b=2)
            DVE(lambda e: e.tensor_tensor(out=lamp[:], in0=lvr[:, :, 0, :], in1=lvr[:, :, 1, :], op=ALU.mult),
                r=[b_pvt], w=[b_lay])
            DVE(lambda e: e.tensor_reduce(out=lam2[:], in_=lamp[:], axis=AX.X, op=ALU.add), r=[b_lay], w=[b_lay])
            ACT(lambda e: e.activation(out=lam2[:], in_=lam2[:], func=AF.Exp), r=[b_lay], w=[b_lay])
            DVE(lambda e: e.tensor_tensor(out=neglam[:], in0=lam2[:, 1:2], in1=lam2[:, 0:1], op=ALU.subtract),
                r=[b_lay], w=[b_lay])
            DVE(lambda e: e.tensor_scalar(out=neglam[:], in0=neglam[:], scalar1=-lam_init, scalar2=None, op0=ALU.add),
                r=[b_lay], w=[b_lay])
            DVE(lambda e: e.tensor_scalar(out=gsub[:], in0=PVT[:, o + PV_SUB:o + PV_SUB + 128], scalar1=(1.0 - lam_init),
                                          scalar2=None, op0=ALU.mult), r=[b_pvt], w=[b_lay])
            DMA(lambda e: e.dma_start(out=wcf32[:], in_=wcf_d[l]), w=[b_lay], dq="lay")
            POOL(lambda e: e.tensor_copy(out=wcfb[:], in_=wcf32[:]), r=[b_lay], w=[b_lay])

        rots = {"s": 0, "w": 0, "acc": 0, "tr": 0, "ds": 0, "g": 0, "p": 0, "y": 0, "ip": 0}

        def inproj_gen(l, n, kind, idx, qs_, need_barrier=True):
            QTp, KT, VE, bQK, bVE = QTp_s[qs_], KT_s[qs_], VE_s[qs_], bQK_s[qs_], bVE_s[qs_]
            ds = rots["ds"] % 2
            rots["ds"] += 1
            wv = WB[:, 2 * ds:2 * ds + 2, :].rearrange("p a b -> p (a b)")[:, 0:3072].rearrange("p (k c) -> p k c", k=KC)
            bw = bWBh[2 * ds] + bWBh[2 * ds + 1]
            for part in range(3):
                k = load_chunk(l, cid_inp(n, part))
                cast_chunk(k, wv[:, :, part * P:(part + 1) * P], lambda v: v.rearrange("p (k c) -> p k c", k=KC), bw)
            if kind == "A":
                POOL(lambda e: e.memset(VE[:, :, 128:130], 1.0), w=bVE)
            else:
                POOL(lambda e: e.memset(VE[:, :, 64:65], 1.0), w=bVE)
                POOL(lambda e: e.memset(VE[:, :, 129:130], 1.0), w=bVE)
            gain = {"A": gA, "C": gC, "D": gD}.get(kind)
            IPB = [6]
            st_ = {}

            def tile_stages(t):
                b = IPB[0]
                qs, bq = qkvrot.next()
                qb, bqb = qnrot.next()
                s4, bs4 = ss4rot.next()
                tb, btb = TRB, bPB[7]

                def f0():
                    for kc in range(KC):
                        MM(PB[b][:, 0:384], hT[:, kc, t * P:(t + 1) * P], wv[:, kc, :], (kc == 0), (kc == KC - 1),
                           [bhT[t]] + bw, [bPB[b]])

                def f1():
                    S.issue("act", lambda e: e.copy(out=qs, in_=PB[b][:, 0:384]), [bPB[b]], [bq])

                def f2():
                    if kind != "B":
                        TT("dve", sqt, qs[:, 0:256], qs[:, 0:256], ALU.mult, [bq], [bsq])
                        S.issue("dve", lambda e: e.tensor_reduce(out=s4, in_=sqt.rearrange("p (g d) -> p g d", g=4),
                                                                 axis=AX.X, op=ALU.add), [bsq], [bs4])
                    if kind == "A":
                        CP("pool", VE[:, t, 0:128], qs[:, 256:384], [bq], [bVE[t]])
                    else:
                        CP("pool", VE[:, t, :].rearrange("p (m w) -> p m w", m=2)[:, :, 0:64],
                           qs[:, 256:384].rearrange("p (m d) -> p m d", m=2), [bq], [bVE[t]])

                def f3():
                    if kind != "B":
                        AV(s4, s4, AF.Ln, [bs4], [bs4], scale=1.0 / 64, bias=EPS)
                        AV(s4, s4, AF.Exp, [bs4], [bs4], scale=-0.5)

                def f4():
                    if kind != "B":
                        TT("dve", qs[:, 0:256].rearrange("p (g d) -> p g d", g=4), qs[:, 0:256].rearrange("p (g d) -> p g d", g=4),
                           s4.unsqueeze(2).to_broadcast([P, 4, 64]), ALU.mult, [bq, bs4], [bq])
                        TT("dve", qb, qs[:, 0:256], gain[:], ALU.mult, [bq, b_lay], [bqb])
                    else:
                        TT("dve", qb, qs[:, 0:256], sBt[:], ALU.mult, [bq, b_cst], [bqb])

                def f5():
                    TRN(tb[:, 0:P], qb[:, 0:128], [bqb, b_ident], [btb])
                    TRN(tb[:, P:2 * P], qb[:, 128:256], [bqb, b_ident], [btb])

                def f6():
                    CP("dve", QTp[0][0:64, t * P:(t + 1) * P], tb[0:64, 0:P], [btb], [bQK[t]])
                    CP("dve", QTp[1][64:128, t * P:(t + 1) * P], tb[64:128, 0:P], [btb], [bQK[t]])
                    CP("dve", KT[:, t * P:(t + 1) * P], tb[:, P:2 * P], [btb], [bQK[t]])

                return [f0, f1, f2, f3, f4, f5, f6]

            inflight = []
            nt_ = 0
            while nt_ < NT or inflight:
                if len(inflight) < 2 and nt_ < NT:
                    inflight.append([tile_stages(nt_), 0])
                    nt_ += 1
                for item in list(inflight):
                    item[0][item[1]]()
                    item[1] += 1
                    if item[1] == 7:
                        inflight.remove(item)
                    yield "step"

            if kind == "C":
                if need_barrier:
                    yield "barrier"
                if idx == 0:
                    S.alias(ex_C, ex_all)
                for m in range(2):
                    hc = 2 * idx + m
                    POOL(lambda e: e.memset(wrep, 0.0), w=[bwrep])
                    for a in range(4):
                        POOL(lambda e, a=a, hc=hc: e.tensor_copy(out=wrep[:, :, a * 32:a * 32 + 1], in_=wcfb[:, :, hc:hc + 1]),
                             r=[b_lay], w=[bwrep])
                    for cc in range(4):
                        for kc in range(KC):
                            PE(lambda e, kc=kc, cc=cc: e.matmul(PB[6][:, :], lhsT=wrep[:, kc, :],
                                                                rhs=hT[:, kc, cc * 512:(cc + 1) * 512],
                                                                start=(kc == 0), stop=(kc == KC - 1)),
                               r=[bwrep] + bhT[cc * 4:(cc + 1) * 4], w=[bPB[6]])
                        yield "step"
                        ACT(lambda e, hc=hc: e.activation(out=ce32, in_=PB[6][:, :], func=AF.Exp, scale=-1.0,
                                                          bias=negfb[:, hc:hc + 1]), r=[bPB[6], b_lay], w=[bce32])
                        yield "step"
                        ACT(lambda e: e.activation(out=ce32, in_=ce32, func=AF.Ln, bias=1.0), r=[bce32], w=[bce32])
                        yield "step"
                        cn = cn32[cc % 2]
                        bcn = bcn32[cc % 2]
                        init = 0.0 if cc == 0 else cn32[(cc - 1) % 2][:, 511:512]
                        DVE(lambda e, cn=cn, init=init: e.tensor_tensor_scan(
                            out=cn, data0=ones1[:].to_broadcast([P, 512]), data1=ce32, initial=init,
                            op0=ALU.mult, op1=ALU.add), r=[bce32, bcn32[(cc - 1) % 2], b_cst], w=[bcn])
                        yield "step"
                        DVE(lambda e, cn=cn: e.tensor_copy(out=chi, in_=cn), r=[bcn], w=[bchi])
                        DVE(lambda e, cn=cn: e.tensor_tensor(out=clo, in0=cn, in1=chi, op=ALU.subtract), r=[bcn, bchi], w=[bclo])
                        yield "step"
                        DVE(lambda e: e.tensor_scalar(out=ce32, in0=chi, scalar1=cselt[:, 0:1], scalar2=cselt[:, 2:3],
                                                      op0=ALU.mult, op1=ALU.add), r=[bchi, b_cst], w=[bce32])
                        DVE(lambda e, m=m, cc=cc: e.scalar_tensor_tensor(
                            out=QE[m][:, cc * 512:(cc + 1) * 512], in0=clo, scalar=cselt[:, 1:2], in1=ce32,
                            op0=ALU.mult, op1=ALU.add), r=[bclo, bce32, b_cst], w=[bQE[m][cc]])
                        yield "step"
                        DVE(lambda e: e.tensor_scalar(out=ce32, in0=chi, scalar1=cselt[:, 3:4], scalar2=cselt[:, 5:6],
                                                      op0=ALU.mult, op1=ALU.add), r=[bchi, b_cst], w=[bce32])
                        DVE(lambda e, m=m, cc=cc: e.scalar_tensor_tensor(
                            out=KE[m][:, cc * 512:(cc + 1) * 512], in0=clo, scalar=cselt[:, 4:5], in1=ce32,
                            op0=ALU.mult, op1=ALU.add), r=[bclo, bce32, b_cst], w=[bKE[m][cc]])
                        yield "step"

        DMAX = {0: 1, 1: 4, 2: 15}

        accst = sb("accst", [P, 516], F32)
        baccst = Buf("accst")

        def exhaust_gen(g):
            if g is not None:
                for _ in g:
                    pass

        def attention(l, n, kind, idx, qs_, tick):
            W = 129 if kind == "A" else 65
            chunk_o = ot_chunk(kind, idx)
            ev = [None]

            def tick2():
                tick()
                if ev[0] is not None:
                    next(ev[0], None)

            for c in range(4):
                ob, bob = obfrot.next()
                for m in range(2):
                    ev[0] = attn_round(kind, idx, c, m, ob, bob, W, chunk_o, qs_, tick2, ev)
            exhaust_gen(ev[0])

        def attn_round(kind, idx, c, m, ob, bob, W, chunk_o, qs_, tick, ev):
            QTp, KT, VE, bQK, bVE = QTp_s[qs_], KT_s[qs_], VE_s[qs_], bQK_s[qs_], bVE_s[qs_]
            r0 = 64 * m
            v0, v1 = (0, 129) if kind == "A" else (m * 65, m * 65 + 65)
            if kind == "A":
                accb = [4, 4, 5, 5]
                acco = [0, 129, 0, 129]
            else:
                bk = 4 + (rots["acc"] % 2)
                rots["acc"] += 1
                accb = [bk] * 4
                acco = [0, 65, 130, 195]
            units = []
            jlo = max(0, 4 * c - DMAX[idx]) if kind == "D" else 0
            for j in range(jlo, 4 * c + 4):
                i_lo = max(4 * c, j)
                i_hi = 4 * c + 3 if kind != "D" else min(4 * c + 3, j + DMAX[idx])
                units.append((j, i_lo, i_hi))
            if kind == "B":
                units.reverse()
            nU = len(units)
            pvl = [(u, i) for u, (j, i_lo, i_hi) in enumerate(units) for i in range(i_lo, i_hi + 1)]
            first, last, pvidx = {}, {}, {}
            for k, (u, i) in enumerate(pvl):
                bnk = accb[i - 4 * c]
                first.setdefault(bnk, k)
                last[bnk] = k
                pvidx[(u, i)] = k
            state = {}

            def rQ(u):
                j, i_lo, i_hi = units[u]
                return [bQK[j]] + [bQK[i] for i in range(i_lo, i_hi + 1)]

            def score(u, b, with_ext):
                j, i_lo, i_hi = units[u]
                N = (i_hi - i_lo + 1) * P
                q0 = i_lo * P
                extras = []
                for i in range(i_lo, i_hi + 1):
                    dl = i - j
                    off = (i - i_lo) * P
                    if kind == "A" and dl <= 1:
                        extras.append((off, tilesA[:, idx * 2 + dl, :]))
                    elif kind == "B" and dl == 0:
                        extras.append((off, maskB[:]))
                    elif kind == "C" and dl == 0:
                        extras.append((off, maskC[:]))
                    elif kind == "D":
                        extras.append((off, tilesD[:, d_type_index(idx, dl) * 2 + m, :]))
                nmm = 1 + len(extras) + (1 if kind == "C" else 0) + with_ext
                cnt = 1
                MM(PB[b][:, 0:N], KT[:, j * P:(j + 1) * P], QTp[m][:, q0:q0 + N], True, (nmm == 1),
                   rQ(u), [bPB[b]])
                if kind == "C":
                    cnt += 1
                    MM(PB[b][:, 0:N], KE[m][:, j * P:(j + 1) * P], QE[m][:, q0:q0 + N], False, (cnt == nmm),
                       [bKE[m][j // 4], bQE[m][c]], [bPB[b]])
                for (off, tl) in extras:
                    cnt += 1
                    MM(PB[b][:, off:off + P], ident[:], tl, False, (cnt == nmm), [b_ident, b_cst], [bPB[b]])
                return N

            def stage1(u):
                b = rots["s"] % (2 if kind == "B" else 4)
                rots["s"] += 1
                N = score(u, b, 0)
                if kind != "B":
                    pt, bpt = ptrot.next()
                    AV(pt[:, 0:N], PB[b][:, 0:N], AF.Exp, [bPB[b]], [bpt])
                    state[u] = (pt, bpt, N)
                else:
                    ee, be = e32rot.next()
                    lb, blb = lbrot.next()
                    AV(ee[:, 0:N], PB[b][:, 0:N], AF.Exp, [bPB[b]], [be])
                    AV(lb[:, 0:N], ee[:, 0:N], AF.Ln, [be], [blb], bias=1.0)
                    state[u] = (lb, blb, N)

            def stage2(u):
                j, i_lo, i_hi = units[u]
                lb, blb, N = state[u]
                co = (i_lo - 4 * c) * P
                wbk = 2 + (rots["w"] % 2)
                rots["w"] += 1
                score(u, wbk, 1 + (1 if u > 0 else 0))
                MM(PB[wbk][:, 0:N], negTri[:], lb[:, 0:N], False, (u == 0), [blb, b_cst], [bPB[wbk]])
                if u > 0:
                    MM(PB[wbk][:, 0:N], negOnes[:], Lsb[m][u % 2][:, co:512], False, True, [bLsb[m][u % 2], b_cst], [bPB[wbk]])
                pt, bpt = ptrot.next()
                AV(pt[:, 0:N], PB[wbk][:, 0:N], AF.Exp, [bPB[wbk]], [bpt])
                if u < nU - 1:
                    nxt = Lsb[m][(u + 1) % 2]
                    if co > 0:
                        CP("pool", nxt[:, 0:co], Ls32[m][:, 0:co], [bLs32[m]], [bLsb[m][(u + 1) % 2]])
                    TT("dve", nxt[:, co:512], Ls32[m][:, co:512], lb[:, 0:N], ALU.add, [bLs32[m], blb], [bLsb[m][(u + 1) % 2]])
                    TT("dve", Ls32[m][:, co:512], Ls32[m][:, co:512], lb[:, 0:N], ALU.add, [bLs32[m], blb], [bLs32[m]])
                state[u] = (pt, bpt, N)

            def stage3(u):
                j, i_lo, i_hi = units[u]
                pt, bpt, N = state[u]
                for i in range(i_lo, i_hi + 1):
                    k = pvidx[(u, i)]
                    il = i - 4 * c
                    bnk = accb[il]
                    MM(PB[bnk][:, acco[il]:acco[il] + W], pt[:, (i - i_lo) * P:(i - i_lo + 1) * P], VE[:, j, v0:v1],
                       (first[bnk] == k), (last[bnk] == k), [bpt, bVE[j]], [bPB[bnk]], skip=True)

            if kind != "B":
                for step in range(nU + 2):
                    if step < nU:
                        stage1(step)
                    if step >= 2:
                        stage3(step - 2)
                    tick()
            else:
                MSET("dve", Ls32[m], 0.0, [bLs32[m]])
                for step in range(nU + 2):
                    if step < nU:
                        stage1(step)
                    if 1 <= step <= nU:
                        stage2(step - 1)
                    if step >= 2:
                        stage3(step - 2)
                    tick()

            exhaust_gen(ev[0])
            ev[0] = None
            if kind == "A":
                CP("dve", accst[:, 0:258], PB[4][:, 0:258], [bPB[4]], [baccst])
                CP("dve", accst[:, 258:516], PB[5][:, 0:258], [bPB[5]], [baccst])
                accv = [accst[:, (il // 2) * 258 + (il % 2) * 129:(il // 2) * 258 + (il % 2) * 129 + 129] for il in range(4)]
            else:
                CP("dve", accst[:, 0:260], PB[accb[0]][:, 0:260], [bPB[accb[0]]], [baccst])
                accv = [accst[:, il * 65:(il + 1) * 65] for il in range(4)]

            def il_stages(il):
                i = 4 * c + il
                acc = accv[il]
                st = []
                if kind == "A":
                    rl, brl_ = rlrot.next()
                    if m == 0:
                        def a0():
                            RCP(rl, acc[:, 128:129], [baccst], [brl_])
                            TS(A0[:, il, :], acc[:, 0:128], rl, None, ALU.mult, None, [baccst, brl_], [bA0[il]])
                        st.append(a0)
                    else:
                        os_, bos_ = osrot.next()
                        s1_, bs1_ = ss1rot.next()

                        def a1():
                            RCP(rl, acc[:, 128:129], [baccst], [brl_])
                            TS(os_, acc[:, 0:128], rl, None, ALU.mult, None, [baccst, brl_], [bos_])
                            STT(os_, os_, neglam[:, 0:1], A0[:, il, :], ALU.mult, ALU.add, [bos_, bA0[il], b_lay], [bos_])

                        def a2():
                            AV(ob[:, il, :], os_, AF.Square, [bos_], [bs1_, bob], accum=s1_)

                        def a3():
                            AV(s1_, s1_, AF.Ln, [bs1_], [bs1_], scale=1.0 / 128, bias=EPS)
                            AV(s1_, s1_, AF.Exp, [bs1_], [bs1_], scale=-0.5)

                        def a4():
                            STT(ob[:, il, :], os_, s1_, gsub[:], ALU.mult, ALU.mult, [bos_, bs1_, b_lay], [bob])
                        st += [a1, a2, a3, a4]
                elif kind == "B":
                    st.append(lambda: CP("dve", ob[:, il, m * 64:(m + 1) * 64], acc[:, 0:64], [baccst], [bob]))
                elif kind == "C":
                    rl, brl_ = rlrot.next()

                    def c0():
                        RCP(rl, acc[:, 64:65], [baccst], [brl_])
                        TS(ob[:, il, m * 64:(m + 1) * 64], acc[:, 0:64], rl, None, ALU.mult, None, [baccst, brl_], [bob])
                    st.append(c0)
                else:
                    dst = accD[:, i, m * 65:(m + 1) * 65]
                    if idx == 0:
                        st.append(lambda: CP("dve", dst, acc, [baccst], [baccD[i]]))
                    else:
                        st.append(lambda: TT("dve", dst, acc, dst, ALU.add, [baccst, baccD[i]], [baccD[i]]))
                    if idx == 2:
                        rl, brl_ = rlrot.next()

                        def d1():
                            RCP(rl, accD[:, i, m * 65 + 64:m * 65 + 65], [baccD[i]], [brl_])
                            TS(ob[:, il, m * 64:(m + 1) * 64], accD[:, i, m * 65:m * 65 + 64], rl, None, ALU.mult, None,
                               [baccD[i], brl_], [bob])
                        st.append(d1)
                if m == 1 and (kind != "D" or idx == 2):
                    s_ = 4 + (rots["tr"] % 4)
                    rots["tr"] += 1
                    st.append(lambda: TRN(TRB[:, s_ * P:(s_ + 1) * P], ob[:, il, :], [bob, b_ident], [bTR[s_]]))
                    st.append(lambda: CP("dve", oT[:, chunk_o, i * P:(i + 1) * P], TRB[:, s_ * P:(s_ + 1) * P], [bTR[s_]],
                                         [boT[chunk_o][i]]))
                return st

            def evac_gen():
                lists = [il_stages(il) for il in range(4)]
                for pair in ((0, 1), (2, 3)):
                    k = 0
                    while True:
                        did = False
                        for il in pair:
                            if k < len(lists[il]):
                                lists[il][k]()
                                did = True
                        if not did:
                            break
                        k += 1
                        yield "e"

            return evac_gen()

        BR_CHUNKS = {0: [6, 7, 8, 9], 1: [0, 1, 2], 2: [3, 4, 5], 3: [10]}

        def merge(l):
            S.alias(lo_merge, lo_all)
            S.alias(bmgc[1], ex_all)
            S.alias(bmgc[0] + [bmacc], set1_bufs)
            for hf in range(2):
                for dc in range(8):
                    ds = rots["ds"] % 2
                    rots["ds"] += 1
                    bw = bWBh[2 * ds] + bWBh[2 * ds + 1]
                    gv = WB[:, 2 * ds:2 * ds + 2, :].rearrange("p a b -> p (a b)").rearrange("p (i k c) -> p i k c", i=4, k=KC)
                    for i in range(4):
                        k = load_chunk(l, cid_gate(dc, i))
                        cast_chunk(k, gv[:, i], lambda v: v.rearrange("p (k c) -> p k c", k=KC), bw)
                    wbd, bwbd_ = wbdrot.next()
                    k = load_chunk(l, cid_wbr(dc, 0))
                    cast_chunk(k, wbd[:, 0:8, :], lambda v: v.rearrange("p (f c) -> p f c", f=8), [bwbd_])
                    k = load_chunk(l, cid_wbr(dc, 1))
                    cast_chunk(k, wbd[:, 8:11, :], lambda v: v[:, 0:384].rearrange("p (f c) -> p f c", f=3), [bwbd_])
                    for tc in range(2):
                        tok0 = hf * 1024 + tc * 512
                        tls = list(range(tok0 // P, tok0 // P + 4))
                        msl = macc[:, tc * 512:(tc + 1) * 512]
                        for i in range(4):
                            gb_ = rots["g"] % 2
                            rots["g"] += 1
                            for kc in range(KC):
                                MM(PB[gb_][:, :], gv[:, i, kc, :], hT[:, kc, tok0:tok0 + 512], (kc == 0), (kc == KC - 1),
                                   bw + [bhT[t] for t in tls], [bPB[gb_]])
                            gs, bgs = gsbrot.next()
                            AV(gs, PB[gb_][:, :], AF.Sigmoid, [bPB[gb_], b_cst], [bgs], bias=GBT[:, l, i * 8 + dc:i * 8 + dc + 1])
                            pb_ = 2 + (rots["p"] % 2)
                            rots["p"] += 1
                            chs = BR_CHUNKS[i]
                            for q, fc in enumerate(chs):
                                MM(PB[pb_][:, :], wbd[:, fc, :], oT[:, fc, tok0:tok0 + 512], (q == 0), (q == len(chs) - 1),
                                   [bwbd_] + [boT[fc][t] for t in tls], [bPB[pb_]])
                            if i == 0:
                                TT("dve", msl, gs, PB[pb_][:, :], ALU.mult, [bgs, bPB[pb_]], [bmacc])
                            else:
                                tm, btm = tmprot.next()
                                TT("dve", tm, gs, PB[pb_][:, :], ALU.mult, [bgs, bPB[pb_]], [btm])
                                if i < 3:
                                    TT("dve", msl, msl, tm, ALU.add, [bmacc, btm], [bmacc])
                                else:
                                    TT("dve", mgT[hf][:, dc, tc * 512:(tc + 1) * 512], msl, tm, ALU.add, [bmacc, btm],
                                       [bmgc[hf][tc]])
            for hf in range(2):
                for t in range(hf * 8, hf * 8 + 8):
                    if t < 11:
                        S.alias([bX[t]], boT[t])
                    elif t < 15:
                        S.alias([bX[t]], bmgc[0])
                    else:
                        S.alias([bX[t]], [bmacc])
                    DMAX_(XR[:, t, :], xsc_d[t], [], [bX[t]], f"x{t}")
                for ch in range(2):
                    ds = rots["ds"] % 2
                    rots["ds"] += 1
                    bw = bWBh[2 * ds] + bWBh[2 * ds + 1]
                    wv = WB[:, 2 * ds:2 * ds + 2, :].rearrange("p a b -> p (a b)").rearrange("p (d c) -> p d c", d=8)
                    for q in range(4):
                        k = load_chunk(l, cid_wout(ch, q))
                        cast_chunk(k, wv[:, 2 * q:2 * q + 2, :], lambda v: v.rearrange("p (d c) -> p d c", d=2), bw)
                    for tt in range(8):
                        t = hf * 8 + tt
                        yb = 4 + (rots["y"] % 2)
                        rots["y"] += 1
                        for dc in range(8):
                            MM(PB[yb][:, :], mgT[hf][:, dc, tt * P:(tt + 1) * P], wv[:, dc, :], (dc == 0), (dc == 7),
                               [bmgc[hf][tt // 4]] + bw, [bPB[yb]])
                        TT("dve", XR[:, t, ch * 512:(ch + 1) * 512], PB[yb][:, :], XR[:, t, ch * 512:(ch + 1) * 512], ALU.add,
                           [bPB[yb], bX[t]], [bX[t]])

        def mixer(l):
            S.alias(mixer_ar, ar_all)
            S.alias(bhT[8:16], bxn + bsg)
            S.alias(bxn, ex_all)
            S.alias(bxn, ar_all)
            for t in range(NT):
                make_hT(t, t * P, l * 3 + 1, "mix")
            for t in range(NT):
                DMA(lambda e, t=t: e.dma_start(out=xsc_d[t], in_=XR[:, t, :]), r=[bX[t]], dq=f"x{t}")
            for cch in range(11):
                S.alias(boT[cch], [bX[cch]])
            S.alias(bmgc[0], bX[11:15])
            S.alias([bmacc], [bX[15]])
            layer_consts(l)
            S.alias(set1_bufs, bX[11:16])
            for k_ in range(2):
                MSET("pool", QTp_s[k_][0][64:128, :], 0.0, bQK_s[k_])
                MSET("pool", QTp_s[k_][1][0:64, :], 0.0, bQK_s[k_])
            S.alias(ex_B, ex_all)

            def exhaust(g):
                if g is not None:
                    for _ in g:
                        pass

            exhaust(inproj_gen(l, 0, HPUS[0][0], HPUS[0][1], 0))
            for n, (kind, idx) in enumerate(HPUS):
                if kind == "D" and idx == 0:
                    S.alias(baccD, ex_all)
                nxt = (inproj_gen(l, n + 1, HPUS[n + 1][0], HPUS[n + 1][1], (n + 1) % 2, need_barrier=(kind in ("B", "C")))
                       if n + 1 < len(HPUS) else None)
                cnt = [0]
                blocked = [False]
                if kind == "D":
                    jl = lambda c_: max(0, 4 * c_ - DMAX[idx])
                    tot = 2 * sum((4 * c_ + 4 - jl(c_)) + 2 for c_ in range(4))
                else:
                    tot = 96
                cad = 1

                def tick(nxt=nxt, cnt=cnt, blocked=blocked, cad=cad, spt={"B": 2, "D": 2}.get(kind, 1)):
                    cnt[0] += 1
                    if nxt is None or blocked[0] or cnt[0] % cad != 0:
                        return
                    for _ in range(spt):
                        if next(nxt, "done") == "barrier":
                            blocked[0] = True
                            break

                attention(l, n, kind, idx, n % 2, tick)
                exhaust(nxt)
            if debug and l == 0:
                DMAX_(dbg_d, XRf[:, 0:11264], [b for row in boT for b in row], [], "dbg")
            merge(l)

        for b in range(nb):
            for t in range(NT):
                DMA(lambda e, b=b, t=t: e.dma_start(out=XR[:, t, :], in_=x_d[b, t * P:(t + 1) * P, :]),
                    w=[bX[t]], dq=f"x{t}")
            done = False
            for l in range(layers):
                ffn(l, 0)
                if stop == "ffn1":
                    done = True
                    break
                mixer(l)
                if stop == "mix":
                    done = True
                    break
                ffn(l, 1)
            for t in range(NT):
                DMA(lambda e, b=b, t=t: e.dma_start(out=out_d[b, t * P:(t + 1) * P, :], in_=XR[:, t, :]),
                    r=[bX[t]], dq=f"x{t}")
        S.emit(final_dqs=[f"x{t}" for t in range(NT)] + (["dbg"] if debug else []))
    return nc


def prep_shared(inp):
    L = DEPTH
    ws = np.zeros((L, NCH, P, 1024), np.float32)

    def kcm(w, c0, n=128):
        return w[:, c0:c0 + n].reshape(KC, P, n).transpose(1, 0, 2)

    for l in range(L):
        for j, (wi, wo) in enumerate(((inp["ffn1_w_in"], inp["ffn1_w_out"]), (inp["ffn2_w_in"], inp["ffn2_w_out"]))):
            for i in range(FC):
                for gu in range(2):
                    ws[l, cid_ffn(j, i, gu)] = kcm(wi[l], gu * FH + i * 128).reshape(P, 1024)
            for r in range(FC):
                ws[l, cid_ffo(j, r)] = wo[l, r * 128:(r + 1) * 128, :]
        win = inp["w_in"][l]
        for n, (kind, idx) in enumerate(HPUS):
            for part, c0 in enumerate(hpu_cols(kind, idx)):
                ws[l, cid_inp(n, part)] = kcm(win, c0).reshape(P, 1024)
        for dc in range(8):
            for i in range(4):
                ws[l, cid_gate(dc, i)] = kcm(win, OFF_GL + i * 1024 + dc * 128).reshape(P, 1024)
        wbr = inp["w_branch"][l]
        rows = []
        for (kind, idx) in [("B", 0), ("B", 1), ("B", 2), ("C", 0), ("C", 1), ("C", 2), ("A", 0), ("A", 1), ("A", 2), ("A", 3), ("D", 0)]:
            r0 = {"A": 0, "B": 512, "C": 896, "D": 1280}[kind] + idx * 128
            rows.append(wbr[r0:r0 + 128])
        wbr_o = np.stack(rows)
        for dc in range(8):
            blk = wbr_o[:, :, dc * 128:(dc + 1) * 128].transpose(1, 0, 2)
            ws[l, cid_wbr(dc, 0)] = blk[:, 0:8].reshape(P, 1024)
            ws[l, cid_wbr(dc, 1), :, 0:384] = blk[:, 8:11].reshape(P, 384)
        wo_ = inp["w_out"][l]
        for half in range(2):
            for q in range(4):
                blk = wo_[q * 256:(q + 1) * 256, half * 512:(half + 1) * 512].reshape(2, P, 512).transpose(1, 0, 2)
                ws[l, cid_wout(half, q)] = blk.reshape(P, 1024)
    gT = np.stack([np.stack([inp[k][l].reshape(KC, P).T for k in ("ffn1_norm", "mix_norm", "ffn2_norm")]) for l in range(L)])
    gb = np.stack([inp["gate_bias"][l].reshape(4, 8, P).transpose(2, 0, 1).reshape(P, 32) for l in range(L)])
    pv = np.zeros((L, NPV), np.float32)
    for l in range(L):
        pv[l, PV_FB:PV_FB + 6] = inp["forget_bias"][l]
        pv[l, PV_AQ:PV_AQ + 64] = inp["a_q_norm"][l]
        pv[l, PV_AK:PV_AK + 64] = inp["a_k_norm"][l]
        pv[l, PV_LAM:PV_LAM + 256] = inp["a_lambda"][l].reshape(-1)
        pv[l, PV_SUB:PV_SUB + 128] = inp["a_subln"][l]
        pv[l, PV_CQ:PV_CQ + 64] = inp["c_q_norm"][l]
        pv[l, PV_CK:PV_CK + 64] = inp["c_k_norm"][l]
        pv[l, PV_DQ:PV_DQ + 64] = inp["d_q_norm"][l]
        pv[l, PV_DK:PV_DK + 64] = inp["d_k_norm"][l]
        pv[l, PV_REL31:PV_REL31 + 10] = inp["rel_table"][31]
    wcf = np.stack([inp["w_in"][l][:, OFF_CF:OFF_CF + 6].reshape(KC, P, 6).transpose(1, 0, 2) for l in range(L)])
    rel = inp["rel_table"]
    kk = np.arange(P)[:, None]
    qq = np.arange(P)[None, :]
    biasA = np.zeros((P, 8, P), np.float32)
    for h in range(4):
        for dl in range(2):
            biasA[:, h * 2 + dl, :] = rel[rel_bucket_np(np.maximum(128 * dl + qq - kk, 0)), h]
    biasD = np.zeros((P, 18, P), np.float32)
    masks = np.zeros((P, NMASK, P), np.float32)
    for ti, (g, ty) in enumerate(D_TYPES):
        window, dil = D_PAIRS[g]
        dist = 128 * d_type_dist(g, ty) + qq - kk
        valid = (dist >= 0) & (dist <= window) & (dist % dil == 0)
        if ty == "far":
            valid = (dist % dil == 0)
        masks[:, ti, :] = np.where(valid, 0.0, NEG)
        for hh in range(2):
            biasD[:, ti * 2 + hh, :] = rel[rel_bucket_np(np.maximum(dist, 0)), 4 + 2 * g + hh]
    masks[:, M_STRICT, :] = np.where(kk < qq, 0.0, NEG)
    csel = np.zeros((P, 8), np.float32)
    csel[0, 0] = -1.0
    csel[64, 1] = -1.0
    csel[32, 2] = 1.0
    csel[96, 2] = 1.0
    csel[32, 3] = 1.0
    csel[96, 4] = 1.0
    csel[0, 5] = 1.0
    csel[64, 5] = 1.0
    return {"wstream": ws, "gT": np.ascontiguousarray(gT), "gbias": np.ascontiguousarray(gb),
            "pvec": pv.reshape(1, -1), "wcf": np.ascontiguousarray(wcf), "biasA": biasA, "biasD": biasD,
            "masks": masks, "csel": csel}


def kernel(**inputs):
    inp = {k: np.asarray(v, dtype=np.float32) for k, v in inputs.items()}
    shared = prep_shared(inp)
    n = 8
    nb = inp["x"].shape[0] // n
    nc = build_program(nb=nb, layers=DEPTH)
    in_maps = []
    for c in range(n):
        m = dict(shared)
        m["x"] = np.ascontiguousarray(inp["x"][c * nb:(c + 1) * nb])
        in_maps.append(m)
    res = run_bass_kernel_spmd(nc, in_maps, core_ids=list(range(n)))
    return np.concatenate([r["out"] for r in res.results], axis=0).astype(np.float32)
```

```python
import numpy as np
import math
import concourse.bass as bass
import concourse.mybir as mybir
from concourse.bass_utils import run_bass_kernel_spmd
from contextlib import ExitStack

F32 = mybir.dt.float32
BF16 = mybir.dt.bfloat16
AF = mybir.ActivationFunctionType
ALU = mybir.AluOpType
AX = mybir.AxisListType

COMPUTE = ("pe", "act", "dve", "pool")
ENGS = ("pe", "act", "dve", "pool", "sp")

P = 128
SEQ = 2048
NT = 16
DM = 1024
KC = 8
FH = 2816
FC = 22
DEPTH = 2
NEG = -30000.0
EPS = 1e-6
SCALE = 0.125

OFF_AQ, OFF_AK, OFF_AV = 0, 512, 1024
OFF_BQ, OFF_BK, OFF_BV = 1536, 1920, 2304
OFF_CQ, OFF_CK, OFF_CV, OFF_CF = 2688, 3072, 3456, 3840
OFF_DQ, OFF_DK, OFF_DV = 3846, 4230, 4614
OFF_GL = 4998

HPUS = [("B", 0), ("B", 1), ("B", 2), ("C", 0), ("A", 0), ("C", 1), ("A", 1), ("C", 2), ("A", 2), ("A", 3),
        ("D", 0), ("D", 1), ("D", 2)]


def hpu_cols(kind, n):
    if kind == "A":
        return OFF_AQ + n * 128, OFF_AK + n * 128, OFF_AV + n * 128
    if kind == "B":
        return OFF_BQ + n * 128, OFF_BK + n * 128, OFF_BV + n * 128
    if kind == "C":
        return OFF_CQ + n * 128, OFF_CK + n * 128, OFF_CV + n * 128
    return OFF_DQ + n * 128, OFF_DK + n * 128, OFF_DV + n * 128


def ot_chunk(kind, n):
    return {"B": 0, "C": 3, "A": 6, "D": 10}[kind] + (0 if kind == "D" else n)


def cid_ffn(j, i, gu):
    return j * 44 + i * 2 + gu


def cid_ffo(j, r):
    return 88 + j * 22 + r


def cid_inp(n, part):
    return 132 + n * 3 + part


def cid_gate(dc, i):
    return 171 + dc * 4 + i


def cid_wbr(dc, part):
    return 203 + dc * 2 + part


def cid_wout(half, q):
    return 219 + half * 4 + q


NCH = 227

PV_FB = 0
PV_AQ = 6
PV_AK = 70
PV_LAM = 134
PV_SUB = 390
PV_CQ = 518
PV_CK = 582
PV_DQ = 646
PV_DK = 710
PV_REL31 = 774
NPV = 784

D_TYPES = [(0, 0), (0, 1), (1, 0), (1, 1), (1, "far"), (1, 4), (2, 0), (2, 1), (2, "far")]
D_PAIRS = ((128, 1), (512, 4), (2048, 16))
M_STRICT = 9
M_ZERO = 10
NMASK = 11


def rel_bucket_np(n):
    n = np.asarray(n)
    nf = np.maximum(n, 1).astype(np.float32)
    large = 16 + (np.log(nf / np.float32(16)) / np.float32(math.log(128 / 16)) * np.float32(16)).astype(np.int32)
    large = np.minimum(large, 31)
    return np.where(n < 16, n, large)


def d_type_index(g, delta):
    if g == 0:
        return {0: 0, 1: 1}[delta]
    if g == 1:
        return {0: 2, 1: 3, 2: 4, 3: 4, 4: 5}[delta]
    return 6 if delta == 0 else (7 if delta == 1 else 8)


def d_type_dist(g, ty):
    return {0: 0, 1: 1, "far": 2, 4: 4}[ty]


class Buf:
    __slots__ = ("name", "w", "r", "excl")

    def __init__(self, name, excl=False):
        self.name = name
        self.w = {}
        self.r = {}
        self.excl = excl


class Op:
    __slots__ = ("fn", "waits", "inc", "dq")

    def __init__(self, fn, waits, dq):
        self.fn = fn
        self.waits = waits
        self.inc = False
        self.dq = dq


class Sched:
    def __init__(self, nc, stack):
        self.nc = nc
        self.stack = stack
        self.ops = {e: [] for e in ENGS}
        self.known = {e: {} for e in ENGS}
        self.sems = {}
        for e in COMPUTE:
            self.sems[e] = stack.enter_context(nc.semaphore("s_" + e))
        self.dma_cnt = {}
        self.clocks = {}

    def issue(self, eng, fn, reads=(), writes=(), dq=None):
        deps = {}
        for b in reads:
            for q, i in b.w.items():
                if deps.get(q, -1) < i:
                    deps[q] = i
            if b.excl:
                for q, i in b.r.items():
                    if q != eng and deps.get(q, -1) < i:
                        deps[q] = i
        for b in writes:
            for q, i in b.w.items():
                if eng == "pe" and q == "pe":
                    continue
                if deps.get(q, -1) < i:
                    deps[q] = i
            for q, i in b.r.items():
                if deps.get(q, -1) < i:
                    deps[q] = i
        kn = self.known[eng]
        waits = []
        for q, i in deps.items():
            if kn.get(q, -1) >= i:
                continue
            waits.append((q, i))
            for q2, i2 in self.clocks[(q, i)].items():
                if kn.get(q2, -1) < i2:
                    kn[q2] = i2
            kn[q] = i
            if q in COMPUTE:
                self.ops[q][i].inc = True
        op = Op(fn, waits, dq)
        idx = len(self.ops[eng])
        self.ops[eng].append(op)
        if dq is None:
            tok = (eng, idx)
        else:
            if dq not in self.sems:
                self.sems[dq] = self.stack.enter_context(self.nc.semaphore("d_" + dq))
                self.dma_cnt[dq] = 0
            self.dma_cnt[dq] += 1
            tok = (dq, self.dma_cnt[dq])
        self.clocks[tok] = dict(kn)
        for b in reads:
            if b.r.get(tok[0], -1) < tok[1]:
                b.r[tok[0]] = tok[1]
        for b in writes:
            b.w = {tok[0]: tok[1]}
            b.r = {}
        return tok

    def fence_group(self, bufs, dq):
        for b in bufs:
            b.w = {dq: self.dma_cnt[dq]}

    def alias(self, new_bufs, old_bufs):
        acc = {}
        for o in old_bufs:
            for d in (o.w, o.r):
                for q, i in d.items():
                    if acc.get(q, -1) < i:
                        acc[q] = i
        for n in new_bufs:
            for q, i in acc.items():
                if n.r.get(q, -1) < i:
                    n.r[q] = i

    def emit(self, final_dqs=()):
        nc = self.nc
        semval = {}
        for e in COMPUTE:
            c = 0
            vals = []
            for op in self.ops[e]:
                if op.inc:
                    c += 1
                vals.append(c)
            semval[e] = vals

        def run(engname, e):
            for op in self.ops[engname]:
                for (q, i) in op.waits:
                    if q in COMPUTE:
                        e.wait_ge(self.sems[q], semval[q][i])
                    else:
                        e.wait_ge(self.sems[q], 16 * i)
                ins = op.fn(e)
                if op.dq is not None:
                    ins.then_inc(self.sems[op.dq], 16)
                elif op.inc:
                    ins.then_inc(self.sems[engname], 1)
            if engname == "sp":
                for dq in final_dqs:
                    e.wait_ge(self.sems[dq], 16 * self.dma_cnt[dq])

        with nc.Block() as block:
            @block.tensor
            def _(e):
                run("pe", e)

            @block.scalar
            def _(e):
                run("act", e)

            @block.vector
            def _(e):
                run("dve", e)

            @block.gpsimd
            def _(e):
                run("pool", e)

            @block.sync
            def _(e):
                run("sp", e)


class Rot:
    def __init__(self, items):
        self.items = items
        self.k = 0

    def next(self):
        it = self.items[self.k % len(self.items)]
        self.k += 1
        return it


def build_program(nb=2, layers=2, stop=None, debug=False):
    nc = bass.Bass("TRN2", target_bir_lowering=False)
    x_d = nc.dram_tensor("x", [nb, SEQ, DM], F32, kind="ExternalInput").ap()
    ws_d = nc.dram_tensor("wstream", [DEPTH, NCH, P, 1024], F32, kind="ExternalInput").ap()
    gT_d = nc.dram_tensor("gT", [DEPTH, 3, P, KC], F32, kind="ExternalInput").ap()
    gb_d = nc.dram_tensor("gbias", [DEPTH, P, 32], F32, kind="ExternalInput").ap()
    pv_d = nc.dram_tensor("pvec", [1, DEPTH * NPV], F32, kind="ExternalInput").ap()
    wcf_d = nc.dram_tensor("wcf", [DEPTH, P, KC, 6], F32, kind="ExternalInput").ap()
    biasA_d = nc.dram_tensor("biasA", [P, 8, P], F32, kind="ExternalInput").ap()
    biasD_d = nc.dram_tensor("biasD", [P, 18, P], F32, kind="ExternalInput").ap()
    masks_d = nc.dram_tensor("masks", [P, NMASK, P], F32, kind="ExternalInput").ap()
    csel_d = nc.dram_tensor("csel", [P, 8], F32, kind="ExternalInput").ap()
    out_d = nc.dram_tensor("out", [nb, SEQ, DM], F32, kind="ExternalOutput").ap()
    dbg_d = nc.dram_tensor("dbg_oT", [P, 11264], F32, kind="ExternalOutput").ap() if debug else None

    with ExitStack() as st:
        S = Sched(nc, st)

        def sb(name, shape, dt):
            return st.enter_context(nc.sbuf_tensor("sb_" + name, shape, dt))

        def psum(name, shape, dt):
            return st.enter_context(nc.psum_tensor(name, shape, dt))

        PE = lambda fn, r=(), w=(): S.issue("pe", fn, r, w)
        ACT = lambda fn, r=(), w=(): S.issue("act", fn, r, w)
        DVE = lambda fn, r=(), w=(): S.issue("dve", fn, r, w)
        POOL = lambda fn, r=(), w=(): S.issue("pool", fn, r, w)
        DMA = lambda fn, r=(), w=(), dq=None: S.issue("sp", fn, r, w, dq)

        def MM(out, lhsT, rhs, start, stop, r, w, skip=False):
            PE(lambda e: e.matmul(out, lhsT=lhsT, rhs=rhs, start=start, stop=stop, skip_group_check=skip), r, w)

        def TRN(out, in_, r, w):
            PE(lambda e: e.transpose(out=out, in_=in_, identity=ident[:]), r, w)

        def AV(out, in_, func, r, w, scale=None, bias=None, accum=None):
            kw = {}
            if scale is not None:
                kw["scale"] = scale
            if bias is not None:
                kw["bias"] = bias
            if accum is not None:
                kw["accum_out"] = accum
            ACT(lambda e: e.activation(out=out, in_=in_, func=func, **kw), r, w)

        def TT(eng, out, in0, in1, op, r, w):
            S.issue(eng, lambda e: e.tensor_tensor(out=out, in0=in0, in1=in1, op=op), r, w)

        def TS(out, in0, s1, s2, op0, op1, r, w):
            if op1 is None:
                DVE(lambda e: e.tensor_scalar(out=out, in0=in0, scalar1=s1, scalar2=None, op0=op0), r, w)
            else:
                DVE(lambda e: e.tensor_scalar(out=out, in0=in0, scalar1=s1, scalar2=s2, op0=op0, op1=op1), r, w)

        def STT(out, in0, scalar, in1, op0, op1, r, w):
            DVE(lambda e: e.scalar_tensor_tensor(out=out, in0=in0, scalar=scalar, in1=in1, op0=op0, op1=op1), r, w)

        def CP(eng, out, in_, r, w):
            S.issue(eng, lambda e: e.tensor_copy(out=out, in_=in_), r, w)

        def RCP(out, in_, r, w):
            DVE(lambda e: e.reciprocal(out=out, in_=in_), r, w)

        def MSET(eng, ap, val, w):
            S.issue(eng, lambda e: e.memset(ap, val), (), w)

        def DMAX_(out, in_, r, w, dq):
            DMA(lambda e: e.dma_start(out=out, in_=in_), r, w, dq)

        XR = sb("XR", [P, NT, DM], F32)
        bX = [Buf(f"X{t}") for t in range(NT)]
        hT = sb("hT", [P, KC, SEQ], BF16)
        bhT = [Buf(f"hT{t}") for t in range(NT)]
        AR = sb("AR", [P, 30720], BF16)
        NSTG = 4
        STG = sb("STG", [P, NSTG, 1024], F32)
        bSTG = [Buf(f"stg{k}") for k in range(NSTG)]
        WB = sb("WB", [P, 4, 2048], BF16)
        bWBh = [[Buf(f"wb{k}a"), Buf(f"wb{k}b")] for k in range(4)]
        bWB = None
        ident = sb("ident", [P, P], BF16)
        identf = sb("identf", [P, P], F32)
        b_ident = Buf("ident")
        gT = sb("gT", [P, DEPTH * 3, KC], F32)
        b_gT = Buf("gT")
        junk_v = {"ffn": hT[:, 7, 1024:2048], "mix": AR[:, 10272:11296]}
        xn_v = {"ffn": [hT[:, 5, 1024:2048], hT[:, 6, 1024:2048]],
                "mix": [AR[:, 8224:9248], AR[:, 9248:10272]]}
        junk_e = AR[:, 26656:26784]
        bxn = [Buf("xn0"), Buf("xn1")]
        ssr = sb("ssr", [P, 2], F32)
        bss = [Buf("ss0"), Buf("ss1")]
        ssrot = Rot([(ssr[:, k:k + 1], bss[k]) for k in range(2)])
        xnk = [0]

        PB = [psum(f"pb{k}", [P, 512], F32) for k in range(8)]
        bPB = [Buf(f"pb{k}", excl=True) for k in range(8)]
        TRB = PB[7].bitcast(BF16)
        TRB6 = PB[6].bitcast(BF16)
        bTR = [bPB[7]] * 8

        DMA(lambda e: e.dma_start(out=gT[:], in_=gT_d.rearrange("l j p k -> p (l j) k")), w=[b_gT], dq="cst")
        POOL(lambda e: e.memset(identf[:], 1.0), w=[b_ident])
        POOL(lambda e: e.affine_select(out=identf[:], in_=identf[:], pattern=[[1, P]], compare_op=ALU.is_equal,
                                       fill=0.0, base=0, channel_multiplier=-1), r=[b_ident], w=[b_ident])
        POOL(lambda e: e.tensor_copy(out=ident[:], in_=identf[:]), r=[b_ident], w=[b_ident])

        stg_k = [0]

        def load_chunk(l, cid):
            k = stg_k[0] % NSTG
            stg_k[0] += 1
            DMA(lambda e: e.dma_start(out=STG[:, k, :], in_=ws_d[l, cid]), w=[bSTG[k]], dq=f"stg{k}")
            return k

        def cast_chunk(k, out_ap, in_view, wbufs, eng="pool"):
            src = in_view(STG[:, k, :])
            if eng == "act":
                ACT(lambda e: e.copy(out=out_ap, in_=src), r=[bSTG[k]], w=wbufs)
            else:
                POOL(lambda e: e.tensor_copy(out=out_ap, in_=src), r=[bSTG[k]], w=wbufs)

        def make_hT(t, col0, gidx, phase):
            ss, bs = ssrot.next()
            xt, bx = xn_v[phase][xnk[0] % 2], bxn[xnk[0] % 2]
            xnk[0] += 1
            ACT(lambda e: e.activation(out=xt, in_=XR[:, t, :], func=AF.Square, accum_out=ss), r=[bX[t]], w=[bs, bx])
            ACT(lambda e: e.activation(out=ss, in_=ss, func=AF.Ln, scale=1.0 / DM, bias=EPS), r=[bs], w=[bs])
            ACT(lambda e: e.activation(out=ss, in_=ss, func=AF.Exp, scale=-0.5), r=[bs], w=[bs])
            DVE(lambda e: e.tensor_scalar(out=xt, in0=XR[:, t, :], scalar1=ss, scalar2=None, op0=ALU.mult),
                r=[bX[t], bs], w=[bx])
            for kc in range(KC):
                PE(lambda e, kc=kc: e.transpose(out=TRB[:, kc * P:(kc + 1) * P], in_=xt[:, kc * P:(kc + 1) * P],
                                                identity=ident[:]), r=[bx, b_ident], w=[bTR[kc]])
            DVE(lambda e: e.tensor_tensor(out=hT[:, :, col0:col0 + P],
                                          in0=TRB.rearrange("p (k c) -> p k c", k=KC),
                                          in1=gT[:, gidx, :].unsqueeze(2).to_broadcast([P, KC, P]), op=ALU.mult),
                r=[bPB[7], b_gT], w=[bhT[col0 // P]])

        actT = AR[:, 0:22528].rearrange("p (i t) -> p i t", i=FC)
        bact = [[Buf(f"act{i}_{c}") for c in range(2)] for i in range(FC)]
        WoG = [AR[:, 22528 + g * 4096: 22528 + (g + 1) * 4096].rearrange("p (r c) -> p r c", r=4) for g in range(2)]
        bWoG = [Buf("wog0"), Buf("wog1")]
        bsg = [Buf("sg0"), Buf("sg1")]
        sgrot = Rot([(hT[:, 3 + k, 1024:2048].bitcast(F32), bsg[k]) for k in range(2)])
        wbrot = [0]
        pbrot = [0]
        GROUPS = [(r, min(r + 4, FC)) for r in range(0, FC, 4)]

        def ffn(l, j):
            S.alias(ffn_ar, ar_all)
            S.alias(bxn + bsg, bhT[8:16])
            gidx = l * 3 + (0 if j == 0 else 2)
            NG = len(GROUPS)

            def prep_in(i):
                s_ = i % 4
                for gu in range(2):
                    k = load_chunk(l, cid_ffn(j, i, gu))
                    cast_chunk(k, WB[:, s_, gu * 1024:(gu + 1) * 1024], lambda v: v, [bWBh[s_][gu]],
                               eng=("act" if gu == 0 else "pool"))

            def prep_out(g):
                r0, r1 = GROUPS[g]
                for r in range(r0, r1):
                    k = load_chunk(l, cid_ffo(j, r))
                    cast_chunk(k, WoG[g % 2][:, r - r0, :], lambda v: v, [bWoG[g % 2]],
                               eng=("act" if r % 2 == 0 else "pool"))

            for ps in range(2):
                prep_in(0)
                prep_in(1)
                for tt in range(8):
                    make_hT(ps * 8 + tt, tt * P, gidx, "ffn")
                for i in range(FC):
                    if i + 2 < FC:
                        prep_in(i + 2)
                    if i == FC - 3:
                        prep_out(0)
                    s_ = i % 4
                    for c in range(2):
                        banks = []
                        for gu in range(2):
                            b = pbrot[0] % 4
                            pbrot[0] += 1
                            banks.append(b)
                            for kc in range(KC):
                                MM(PB[b][:], WB[:, s_, gu * 1024 + kc * P: gu * 1024 + (kc + 1) * P],
                                   hT[:, kc, c * 512:(c + 1) * 512], (kc == 0), (kc == KC - 1),
                                   [bWBh[s_][gu]] + bhT[c * 4:(c + 1) * 4], [bPB[b]])
                        sg, bs_ = sgrot.next()
                        AV(sg, PB[banks[0]][:], AF.Silu, [bPB[banks[0]]], [bs_])
                        TT("dve", actT[:, i, c * 512:(c + 1) * 512], sg, PB[banks[1]][:], ALU.mult,
                           [bs_, bPB[banks[1]]], [bact[i][c]])
                for g, (r0, r1) in enumerate(GROUPS):
                    if g + 1 < NG:
                        prep_out(g + 1)
                    wg = WoG[g % 2]
                    bwg = bWoG[g % 2]
                    for tt in range(8):
                        t = ps * 8 + tt
                        for dh in range(2):
                            b = 4 + (pbrot[0] % 2)
                            pbrot[0] += 1
                            for r in range(r0, r1):
                                MM(PB[b][:], actT[:, r, tt * P:(tt + 1) * P], wg[:, r - r0, dh * 512:(dh + 1) * 512],
                                   (r == r0), (r == r1 - 1), [bact[r][tt // 4], bwg], [bPB[b]])
                            STT(XR[:, t, dh * 512:(dh + 1) * 512], PB[b][:], 0.5, XR[:, t, dh * 512:(dh + 1) * 512],
                                ALU.mult, ALU.add, [bPB[b], bX[t]], [bX[t]])

        xsc_d = nc.dram_tensor("xscratch", [NT, P, DM], F32, kind="Internal").ap()
        XRb = XR[:].rearrange("p t d -> p (t d)").bitcast(BF16)
        XRf = XR[:].rearrange("p t d -> p (t d)")
        oT = XRb[:, 0:22528].rearrange("p (c t) -> p c t", c=11)
        boT = [[Buf(f"oT{c}_{t}") for t in range(NT)] for c in range(11)]
        mgT = [XRb[:, 22528:30720].rearrange("p (c t) -> p c t", c=8),
               AR[:, 8224:16416].rearrange("p (c t) -> p c t", c=8)]
        bmgc = [[Buf(f"mg{h}_{tc}") for tc in range(2)] for h in range(2)]
        macc = XR[:, 15, :]
        bmacc = Buf("macc")

        QTp_s = [[AR[:, 0:2048], AR[:, 28672:30720]], [XRb[:, 22528:24576], XRb[:, 24576:26624]]]
        KT_s = [AR[:, 2048:4096], XRb[:, 26624:28672]]
        VE_s = [AR[:, 4096:6176].rearrange("p (t w) -> p t w", t=NT),
                XRb[:, 28672:30752].rearrange("p (t w) -> p t w", t=NT)]
        bQK_s = [[Buf(f"qk{k}_{t}") for t in range(NT)] for k in range(2)]
        bVE_s = [[Buf(f"ve{k}_{t}") for t in range(NT)] for k in range(2)]
        bQK = bQK_s[0]
        bVE = bVE_s[0]
        set1_bufs = bQK_s[1] + bVE_s[1]
        PTb = [Buf(f"pt{k}") for k in range(4)]
        ptrot = Rot([(AR[:, 6176 + k * 512: 6176 + (k + 1) * 512], PTb[k]) for k in range(4)])
        bE32 = [Buf("e32_0"), Buf("e32_1")]
        e32rot = Rot([(AR[:, 8224 + k * 1024: 8224 + (k + 1) * 1024].bitcast(F32), bE32[k]) for k in range(2)])
        bLb = [Buf("lb0"), Buf("lb1")]
        lbrot = Rot([(AR[:, 10272 + k * 512: 10272 + (k + 1) * 512], bLb[k]) for k in range(2)])
        Ls32 = [AR[:, 11296 + m * 1024: 11296 + (m + 1) * 1024].bitcast(F32) for m in range(2)]
        bLs32 = [Buf("ls32_0"), Buf("ls32_1")]
        Lsb = [[AR[:, 13344 + (m * 2 + pp) * 512: 13344 + (m * 2 + pp + 1) * 512] for pp in range(2)] for m in range(2)]
        bLsb = [[Buf(f"lsb{m}{pp}") for pp in range(2)] for m in range(2)]
        QE = [AR[:, 8224 + m * 2048: 8224 + (m + 1) * 2048] for m in range(2)]
        KE = [AR[:, 12320 + m * 2048: 12320 + (m + 1) * 2048] for m in range(2)]
        bQE = [[Buf(f"qe{m}{c}") for c in range(4)] for m in range(2)]
        bKE = [[Buf(f"ke{m}{c}") for c in range(4)] for m in range(2)]
        accD = AR[:, 8224:8224 + 4160].bitcast(F32).rearrange("p (t w) -> p t w", t=NT)
        baccD = [Buf(f"accD{t}") for t in range(NT)]
        ex_B = bE32 + bLb + bLs32 + bLsb[0] + bLsb[1]
        ex_C = bQE[0] + bQE[1] + bKE[0] + bKE[1]
        ex_all = ex_B + ex_C + baccD + bmgc[1] + bxn
        ce32 = AR[:, 16416:17440].bitcast(F32)
        bce32 = Buf("ce32")
        cn32 = [AR[:, 17440 + k * 1024: 17440 + (k + 1) * 1024].bitcast(F32) for k in range(2)]
        bcn32 = [Buf("cn0"), Buf("cn1")]
        chi = AR[:, 19488:20000]
        clo = AR[:, 20000:20512]
        bchi = Buf("chi")
        bclo = Buf("clo")
        wrep = AR[:, 20512:21536].rearrange("p (k c) -> p k c", k=KC)
        bwrep = Buf("wrep")
        bqkv = [Buf("qkv0"), Buf("qkv1")]
        qkvrot = Rot([(AR[:, 21536 + k * 768: 21536 + (k + 1) * 768].bitcast(F32), bqkv[k]) for k in range(2)])
        sqt = AR[:, 23072:23584].bitcast(F32)
        bsq = Buf("sq")
        bqn = [Buf(f"qn{k}") for k in range(4)]
        qnrot = Rot([(AR[:, 26784 + k * 256: 26784 + (k + 1) * 256], bqn[k]) for k in range(4)])
        A0 = AR[:, 24096:25120].bitcast(F32).rearrange("p (i d) -> p i d", i=4)
        bA0 = [Buf(f"A0_{i}") for i in range(4)]
        bos = [Buf("os0"), Buf("os1")]
        osrot = Rot([(AR[:, 25120 + k * 256: 25120 + (k + 1) * 256].bitcast(F32), bos[k]) for k in range(2)])
        bob4 = [Buf("ob0"), Buf("ob1")]
        obfrot = Rot([(AR[:, 25632 + k * 512: 25632 + (k + 1) * 512].rearrange("p (i d) -> p i d", i=4), bob4[k]) for k in range(2)])
        bgsb = [Buf("gsb0"), Buf("gsb1")]
        gsbrot = Rot([(AR[:, k * 512:(k + 1) * 512], bgsb[k]) for k in range(2)])
        btmpm = [Buf("tmpm0"), Buf("tmpm1")]
        tmprot = Rot([(AR[:, 1024 + k * 1024: 1024 + (k + 1) * 1024].bitcast(F32), btmpm[k]) for k in range(2)])
        bwbd = [Buf("wbd0"), Buf("wbd1")]
        wbdrot = Rot([(AR[:, 3072 + k * 1408: 3072 + (k + 1) * 1408].rearrange("p (f c) -> p f c", f=11), bwbd[k]) for k in range(2)])
        lo_attn = bQK + bVE + PTb
        lo_merge = bgsb + btmpm + bwbd
        lo_all = lo_attn + lo_merge
        mixer_ar = (lo_all + ex_all + [bce32, bchi, bclo, bwrep, bsq] + bcn32 + bqkv + bqn + bA0 + bos + bob4)
        ffn_ar = [b for row in bact for b in row] + bWoG
        ar_all = mixer_ar + ffn_ar + bxn

        PVT = sb("pvt", [P, NPV], F32)
        b_pvt = Buf("pvt")
        GBT = sb("gbt", [P, DEPTH, 32], F32)
        cselt = sb("cselt", [P, 8], F32)
        tilesA = sb("tilesA", [P, 8, P], BF16)
        tilesD = sb("tilesD", [P, 18, P], BF16)
        maskB = sb("maskB", [P, P], BF16)
        maskC = sb("maskC", [P, P], BF16)
        negTri = sb("negTri", [P, P], BF16)
        negOnes = sb("negOnes", [P, P], BF16)
        sBt = sb("sBt", [P, 256], F32)
        ones1 = sb("ones1", [P, 1], F32)
        b_cst = Buf("cst")
        gA = sb("gA", [P, 256], F32)
        gC = sb("gC", [P, 256], F32)
        gD = sb("gD", [P, 256], F32)
        negfb = sb("negfb", [P, 6], F32)
        lamp = sb("lamp", [P, 2, 64], F32)
        lam2 = sb("lam2", [P, 2], F32)
        neglam = sb("neglam", [P, 1], F32)
        gsub = sb("gsub", [P, P], F32)
        wcf32 = sb("wcf32", [P, KC, 6], F32)
        wcfb = sb("wcfb", [P, KC, 6], BF16)
        b_lay = Buf("lay")
        ss4t = sb("ss4t", [P, 2, 4], F32)
        bss4 = [Buf("ss4_0"), Buf("ss4_1")]
        ss4rot = Rot([(ss4t[:, k, :], bss4[k]) for k in range(2)])
        rlt = sb("rlt", [P, 4], F32)
        brl = [Buf(f"rl{k}") for k in range(4)]
        rlrot = Rot([(rlt[:, k:k + 1], brl[k]) for k in range(4)])
        ss1t = sb("ss1t", [P, 2], F32)
        bss1 = [Buf("ss1_0"), Buf("ss1_1")]
        ss1rot = Rot([(ss1t[:, k:k + 1], bss1[k]) for k in range(2)])

        stA = XR[:, 0, :].rearrange("p (a c) -> p a c", a=8)
        stD = XRf[:, 1024:1024 + 2304].rearrange("p (a c) -> p a c", a=18)
        stM = XRf[:, 4096:4096 + NMASK * P].rearrange("p (a c) -> p a c", a=NMASK)
        DMA(lambda e: e.dma_start(out=XR[:, 0, :], in_=biasA_d.rearrange("p a c -> p (a c)")), w=[bX[0]], dq="x0")
        DMA(lambda e: e.dma_start(out=XRf[:, 1024:1024 + 2304], in_=biasD_d.rearrange("p a c -> p (a c)")),
            w=[bX[1], bX[2], bX[3]], dq="x1")
        DMA(lambda e: e.dma_start(out=XRf[:, 4096:4096 + NMASK * P], in_=masks_d.rearrange("p a c -> p (a c)")),
            w=[bX[4], bX[5]], dq="x4")
        DMA(lambda e: e.dma_start(out=PVT[:], in_=pv_d[:, 0:NPV].partition_broadcast(P)), w=[b_pvt], dq="pvt")
        DMA(lambda e: e.dma_start(out=GBT[:], in_=gb_d.rearrange("l p c -> p l c")), w=[b_cst], dq="cst")
        DMA(lambda e: e.dma_start(out=cselt[:], in_=csel_d), w=[b_cst], dq="cst")
        for h in range(4):
            for dl in range(2):
                DVE(lambda e, h=h, dl=dl: e.scalar_tensor_tensor(
                    out=tilesA[:, h * 2 + dl, :], in0=stA[:, h * 2 + dl, :], scalar=PVT[:, PV_REL31 + h:PV_REL31 + h + 1],
                    in1=stM[:, (0 if dl == 0 else M_ZERO), :], op0=ALU.subtract, op1=ALU.add),
                    r=[bX[0], bX[4], bX[5], b_cst, b_pvt], w=[b_cst])
        for ti in range(9):
            for hh in range(2):
                DVE(lambda e, ti=ti, hh=hh: e.tensor_tensor(out=tilesD[:, ti * 2 + hh, :], in0=stD[:, ti * 2 + hh, :],
                                                            in1=stM[:, ti, :], op=ALU.add),
                    r=[bX[1], bX[2], bX[3], bX[4], bX[5]], w=[b_cst])
        DVE(lambda e: e.tensor_copy(out=maskB[:], in_=stM[:, M_STRICT, :]), r=[bX[4], bX[5]], w=[b_cst])
        DVE(lambda e: e.tensor_copy(out=maskC[:], in_=stM[:, 0, :]), r=[bX[4], bX[5]], w=[b_cst])
        POOL(lambda e: e.memset(negOnes[:], -1.0), w=[b_cst])
        POOL(lambda e: e.memset(identf[:], -1.0), r=[b_ident], w=[b_ident])
        POOL(lambda e: e.affine_select(out=identf[:], in_=identf[:], pattern=[[-1, P]], compare_op=ALU.is_ge,
                                       fill=0.0, base=0, channel_multiplier=1), r=[b_ident], w=[b_ident])
        POOL(lambda e: e.tensor_copy(out=negTri[:], in_=identf[:]), r=[b_ident], w=[b_cst])
        POOL(lambda e: e.memset(sBt[:, 0:128], SCALE), w=[b_cst])
        POOL(lambda e: e.memset(sBt[:, 128:256], 1.0), w=[b_cst])
        POOL(lambda e: e.memset(ones1[:], 1.0), w=[b_cst])

        def layer_consts(l):
            o = 0
            DMA(lambda e: e.dma_start(out=PVT[:], in_=pv_d[:, l * NPV:(l + 1) * NPV].partition_broadcast(P)), w=[b_pvt], dq="pvt")
            lam_init = 0.8 - 0.6 * math.exp(-0.3 * l)

            def gains(dst, qo, ko):
                DVE(lambda e: e.tensor_scalar(out=dst[:, 0:128].rearrange("p (a d) -> p a d", a=2),
                                              in0=PVT[:, o + qo:o + qo + 64].unsqueeze(1).to_broadcast([P, 2, 64]),
                                              scalar1=SCALE, scalar2=None, op0=ALU.mult), r=[b_pvt], w=[b_lay])
                DVE(lambda e: e.tensor_copy(out=dst[:, 128:256].rearrange("p (a d) -> p a d", a=2),
                                            in_=PVT[:, o + ko:o + ko + 64].unsqueeze(1).to_broadcast([P, 2, 64])),
                    r=[b_pvt], w=[b_lay])
            gains(gA, PV_AQ, PV_AK)
            gains(gC, PV_CQ, PV_CK)
            gains(gD, PV_DQ, PV_DK)
            DVE(lambda e: e.tensor_scalar(out=negfb[:], in0=PVT[:, o + PV_FB:o + PV_FB + 6], scalar1=-1.0, scalar2=None,
                                          op0=ALU.mult), r=[b_pvt], w=[b_lay])
            lvr = PVT[:, o + PV_LAM:o + PV_LAM + 256].rearrange("p (a b d) -> p a b d", a=2, b=2)
            DVE(lambda e: e.tensor_tensor(out=lamp[:], in0=lvr[:, :, 0, :], in1=lvr[:, :, 1, :], op=ALU.mult),
                r=[b_pvt], w=[b_lay])
            DVE(lambda e: e.tensor_reduce(out=lam2[:], in_=lamp[:], axis=AX.X, op=ALU.add), r=[b_lay], w=[b_lay])
            ACT(lambda e: e.activation(out=lam2[:], in_=lam2[:], func=AF.Exp), r=[b_lay], w=[b_lay])
            DVE(lambda e: e.tensor_tensor(out=neglam[:], in0=lam2[:, 1:2], in1=lam2[:, 0:1], op=ALU.subtract),
                r=[b_lay], w=[b_lay])
            DVE(lambda e: e.tensor_scalar(out=neglam[:], in0=neglam[:], scalar1=-lam_init, scalar2=None, op0=ALU.add),
                r=[b_lay], w=[b_lay])
            DVE(lambda e: e.tensor_scalar(out=gsub[:], in0=PVT[:, o + PV_SUB:o + PV_SUB + 128], scalar1=(1.0 - lam_init),
                                          scalar2=None, op0=ALU.mult), r=[b_pvt], w=[b_lay])
            DMA(lambda e: e.dma_start(out=wcf32[:], in_=wcf_d[l]), w=[b_lay], dq="lay")
            POOL(lambda e: e.tensor_copy(out=wcfb[:], in_=wcf32[:]), r=[b_lay], w=[b_lay])

        rots = {"s": 0, "w": 0, "acc": 0, "tr": 0, "ds": 0, "g": 0, "p": 0, "y": 0, "ip": 0}

        def inproj_gen(l, n, kind, idx, qs_, need_barrier=True):
            QTp, KT, VE, bQK, bVE = QTp_s[qs_], KT_s[qs_], VE_s[qs_], bQK_s[qs_], bVE_s[qs_]
            ds = rots["ds"] % 2
            rots["ds"] += 1
            wv = WB[:, 2 * ds:2 * ds + 2, :].rearrange("p a b -> p (a b)")[:, 0:3072].rearrange("p (k c) -> p k c", k=KC)
            bw = bWBh[2 * ds] + bWBh[2 * ds + 1]
            for part in range(3):
                k = load_chunk(l, cid_inp(n, part))
                cast_chunk(k, wv[:, :, part * P:(part + 1) * P], lambda v: v.rearrange("p (k c) -> p k c", k=KC), bw)
            if kind == "A":
                POOL(lambda e: e.memset(VE[:, :, 128:130], 1.0), w=bVE)
            else:
                POOL(lambda e: e.memset(VE[:, :, 64:65], 1.0), w=bVE)
                POOL(lambda e: e.memset(VE[:, :, 129:130], 1.0), w=bVE)
            gain = {"A": gA, "C": gC, "D": gD}.get(kind)
            IPB = [6]
            st_ = {}

            def tile_stages(t):
                b = IPB[0]
                qs, bq = qkvrot.next()
                qb, bqb = qnrot.next()
                s4, bs4 = ss4rot.next()
                tb, btb = TRB, bPB[7]

                def f0():
                    for kc in range(KC):
                        MM(PB[b][:, 0:384], hT[:, kc, t * P:(t + 1) * P], wv[:, kc, :], (kc == 0), (kc == KC - 1),
                           [bhT[t]] + bw, [bPB[b]])

                def f1():
                    S.issue("act", lambda e: e.copy(out=qs, in_=PB[b][:, 0:384]), [bPB[b]], [bq])

                def f2():
                    if kind != "B":
                        TT("dve", sqt, qs[:, 0:256], qs[:, 0:256], ALU.mult, [bq], [bsq])
                        S.issue("dve", lambda e: e.tensor_reduce(out=s4, in_=sqt.rearrange("p (g d) -> p g d", g=4),
                                                                 axis=AX.X, op=ALU.add), [bsq], [bs4])
                    if kind == "A":
                        CP("pool", VE[:, t, 0:128], qs[:, 256:384], [bq], [bVE[t]])
                    else:
                        CP("pool", VE[:, t, :].rearrange("p (m w) -> p m w", m=2)[:, :, 0:64],
                           qs[:, 256:384].rearrange("p (m d) -> p m d", m=2), [bq], [bVE[t]])

                def f3():
                    if kind != "B":
                        AV(s4, s4, AF.Ln, [bs4], [bs4], scale=1.0 / 64, bias=EPS)
                        AV(s4, s4, AF.Exp, [bs4], [bs4], scale=-0.5)

                def f4():
                    if kind != "B":
                        TT("dve", qs[:, 0:256].rearrange("p (g d) -> p g d", g=4), qs[:, 0:256].rearrange("p (g d) -> p g d", g=4),
                           s4.unsqueeze(2).to_broadcast([P, 4, 64]), ALU.mult, [bq, bs4], [bq])
                        TT("dve", qb, qs[:, 0:256], gain[:], ALU.mult, [bq, b_lay], [bqb])
                    else:
                        TT("dve", qb, qs[:, 0:256], sBt[:], ALU.mult, [bq, b_cst], [bqb])

                def f5():
                    TRN(tb[:, 0:P], qb[:, 0:128], [bqb, b_ident], [btb])
                    TRN(tb[:, P:2 * P], qb[:, 128:256], [bqb, b_ident], [btb])

                def f6():
                    CP("dve", QTp[0][0:64, t * P:(t + 1) * P], tb[0:64, 0:P], [btb], [bQK[t]])
                    CP("dve", QTp[1][64:128, t * P:(t + 1) * P], tb[64:128, 0:P], [btb], [bQK[t]])
                    CP("dve", KT[:, t * P:(t + 1) * P], tb[:, P:2 * P], [btb], [bQK[t]])

                return [f0, f1, f2, f3, f4, f5, f6]

            inflight = []
            nt_ = 0
            while nt_ < NT or inflight:
                if len(inflight) < 2 and nt_ < NT:
                    inflight.append([tile_stages(nt_), 0])
                    nt_ += 1
                for item in list(inflight):
                    item[0][item[1]]()
                    item[1] += 1
                    if item[1] == 7:
                        inflight.remove(item)
                    yield "step"

            if kind == "C":
                if need_barrier:
                    yield "barrier"
                if idx == 0:
                    S.alias(ex_C, ex_all)
                for m in range(2):
                    hc = 2 * idx + m
                    POOL(lambda e: e.memset(wrep, 0.0), w=[bwrep])
                    for a in range(4):
                        POOL(lambda e, a=a, hc=hc: e.tensor_copy(out=wrep[:, :, a * 32:a * 32 + 1], in_=wcfb[:, :, hc:hc + 1]),
                             r=[b_lay], w=[bwrep])
                    for cc in range(4):
                        for kc in range(KC):
                            PE(lambda e, kc=kc, cc=cc: e.matmul(PB[6][:, :], lhsT=wrep[:, kc, :],
                                                                rhs=hT[:, kc, cc * 512:(cc + 1) * 512],
                                                                start=(kc == 0), stop=(kc == KC - 1)),
                               r=[bwrep] + bhT[cc * 4:(cc + 1) * 4], w=[bPB[6]])
                        yield "step"
                        ACT(lambda e, hc=hc: e.activation(out=ce32, in_=PB[6][:, :], func=AF.Exp, scale=-1.0,
                                                          bias=negfb[:, hc:hc + 1]), r=[bPB[6], b_lay], w=[bce32])
                        yield "step"
                        ACT(lambda e: e.activation(out=ce32, in_=ce32, func=AF.Ln, bias=1.0), r=[bce32], w=[bce32])
                        yield "step"
                        cn = cn32[cc % 2]
                        bcn = bcn32[cc % 2]
                        init = 0.0 if cc == 0 else cn32[(cc - 1) % 2][:, 511:512]
                        DVE(lambda e, cn=cn, init=init: e.tensor_tensor_scan(
                            out=cn, data0=ones1[:].to_broadcast([P, 512]), data1=ce32, initial=init,
                            op0=ALU.mult, op1=ALU.add), r=[bce32, bcn32[(cc - 1) % 2], b_cst], w=[bcn])
                        yield "step"
                        DVE(lambda e, cn=cn: e.tensor_copy(out=chi, in_=cn), r=[bcn], w=[bchi])
                        DVE(lambda e, cn=cn: e.tensor_tensor(out=clo, in0=cn, in1=chi, op=ALU.subtract), r=[bcn, bchi], w=[bclo])
                        yield "step"
                        DVE(lambda e: e.tensor_scalar(out=ce32, in0=chi, scalar1=cselt[:, 0:1], scalar2=cselt[:, 2:3],
                                                      op0=ALU.mult, op1=ALU.add), r=[bchi, b_cst], w=[bce32])
                        DVE(lambda e, m=m, cc=cc: e.scalar_tensor_tensor(
                            out=QE[m][:, cc * 512:(cc + 1) * 512], in0=clo, scalar=cselt[:, 1:2], in1=ce32,
                            op0=ALU.mult, op1=ALU.add), r=[bclo, bce32, b_cst], w=[bQE[m][cc]])
                        yield "step"
                        DVE(lambda e: e.tensor_scalar(out=ce32, in0=chi, scalar1=cselt[:, 3:4], scalar2=cselt[:, 5:6],
                                                      op0=ALU.mult, op1=ALU.add), r=[bchi, b_cst], w=[bce32])
                        DVE(lambda e, m=m, cc=cc: e.scalar_tensor_tensor(
                            out=KE[m][:, cc * 512:(cc + 1) * 512], in0=clo, scalar=cselt[:, 4:5], in1=ce32,
                            op0=ALU.mult, op1=ALU.add), r=[bclo, bce32, b_cst], w=[bKE[m][cc]])
                        yield "step"

        DMAX = {0: 1, 1: 4, 2: 15}

        accst = sb("accst", [P, 516], F32)
        baccst = Buf("accst")

        def exhaust_gen(g):
            if g is not None:
                for _ in g:
                    pass

        def attention(l, n, kind, idx, qs_, tick):
            W = 129 if kind == "A" else 65
            chunk_o = ot_chunk(kind, idx)
            ev = [None]

            def tick2():
                tick()
                if ev[0] is not None:
                    next(ev[0], None)

            for c in range(4):
                ob, bob = obfrot.next()
                for m in range(2):
                    ev[0] = attn_round(kind, idx, c, m, ob, bob, W, chunk_o, qs_, tick2, ev)
            exhaust_gen(ev[0])

        def attn_round(kind, idx, c, m, ob, bob, W, chunk_o, qs_, tick, ev):
            QTp, KT, VE, bQK, bVE = QTp_s[qs_], KT_s[qs_], VE_s[qs_], bQK_s[qs_], bVE_s[qs_]
            r0 = 64 * m
            v0, v1 = (0, 129) if kind == "A" else (m * 65, m * 65 + 65)
            if kind == "A":
                accb = [4, 4, 5, 5]
                acco = [0, 129, 0, 129]
            else:
                bk = 4 + (rots["acc"] % 2)
                rots["acc"] += 1
                accb = [bk] * 4
                acco = [0, 65, 130, 195]
            units = []
            jlo = max(0, 4 * c - DMAX[idx]) if kind == "D" else 0
            for j in range(jlo, 4 * c + 4):
                i_lo = max(4 * c, j)
                i_hi = 4 * c + 3 if kind != "D" else min(4 * c + 3, j + DMAX[idx])
                units.append((j, i_lo, i_hi))
            if kind == "B":
                units.reverse()
            nU = len(units)
            pvl = [(u, i) for u, (j, i_lo, i_hi) in enumerate(units) for i in range(i_lo, i_hi + 1)]
            first, last, pvidx = {}, {}, {}
            for k, (u, i) in enumerate(pvl):
                bnk = accb[i - 4 * c]
                first.setdefault(bnk, k)
                last[bnk] = k
                pvidx[(u, i)] = k
            state = {}

            def rQ(u):
                j, i_lo, i_hi = units[u]
                return [bQK[j]] + [bQK[i] for i in range(i_lo, i_hi + 1)]

            def score(u, b, with_ext):
                j, i_lo, i_hi = units[u]
                N = (i_hi - i_lo + 1) * P
                q0 = i_lo * P
                extras = []
                for i in range(i_lo, i_hi + 1):
                    dl = i - j
                    off = (i - i_lo) * P
                    if kind == "A" and dl <= 1:
                        extras.append((off, tilesA[:, idx * 2 + dl, :]))
                    elif kind == "B" and dl == 0:
                        extras.append((off, maskB[:]))
                    elif kind == "C" and dl == 0:
                        extras.append((off, maskC[:]))
                    elif kind == "D":
                        extras.append((off, tilesD[:, d_type_index(idx, dl) * 2 + m, :]))
                nmm = 1 + len(extras) + (1 if kind == "C" else 0) + with_ext
                cnt = 1
                MM(PB[b][:, 0:N], KT[:, j * P:(j + 1) * P], QTp[m][:, q0:q0 + N], True, (nmm == 1),
                   rQ(u), [bPB[b]])
                if kind == "C":
                    cnt += 1
                    MM(PB[b][:, 0:N], KE[m][:, j * P:(j + 1) * P], QE[m][:, q0:q0 + N], False, (cnt == nmm),
                       [bKE[m][j // 4], bQE[m][c]], [bPB[b]])
                for (off, tl) in extras:
                    cnt += 1
                    MM(PB[b][:, off:off + P], ident[:], tl, False, (cnt == nmm), [b_ident, b_cst], [bPB[b]])
                return N

            def stage1(u):
                b = rots["s"] % (2 if kind == "B" else 4)
                rots["s"] += 1
                N = score(u, b, 0)
                if kind != "B":
                    pt, bpt = ptrot.next()
                    AV(pt[:, 0:N], PB[b][:, 0:N], AF.Exp, [bPB[b]], [bpt])
                    state[u] = (pt, bpt, N)
                else:
                    ee, be = e32rot.next()
                    lb, blb = lbrot.next()
                    AV(ee[:, 0:N], PB[b][:, 0:N], AF.Exp, [bPB[b]], [be])
                    AV(lb[:, 0:N], ee[:, 0:N], AF.Ln, [be], [blb], bias=1.0)
                    state[u] = (lb, blb, N)

            def stage2(u):
                j, i_lo, i_hi = units[u]
                lb, blb, N = state[u]
                co = (i_lo - 4 * c) * P
                wbk = 2 + (rots["w"] % 2)
                rots["w"] += 1
                score(u, wbk, 1 + (1 if u > 0 else 0))
                MM(PB[wbk][:, 0:N], negTri[:], lb[:, 0:N], False, (u == 0), [blb, b_cst], [bPB[wbk]])
                if u > 0:
                    MM(PB[wbk][:, 0:N], negOnes[:], Lsb[m][u % 2][:, co:512], False, True, [bLsb[m][u % 2], b_cst], [bPB[wbk]])
                pt, bpt = ptrot.next()
                AV(pt[:, 0:N], PB[wbk][:, 0:N], AF.Exp, [bPB[wbk]], [bpt])
                if u < nU - 1:
                    nxt = Lsb[m][(u + 1) % 2]
                    if co > 0:
                        CP("pool", nxt[:, 0:co], Ls32[m][:, 0:co], [bLs32[m]], [bLsb[m][(u + 1) % 2]])
                    TT("dve", nxt[:, co:512], Ls32[m][:, co:512], lb[:, 0:N], ALU.add, [bLs32[m], blb], [bLsb[m][(u + 1) % 2]])
                    TT("dve", Ls32[m][:, co:512], Ls32[m][:, co:512], lb[:, 0:N], ALU.add, [bLs32[m], blb], [bLs32[m]])
                state[u] = (pt, bpt, N)

            def stage3(u):
                j, i_lo, i_hi = units[u]
                pt, bpt, N = state[u]
                for i in range(i_lo, i_hi + 1):
                    k = pvidx[(u, i)]
                    il = i - 4 * c
                    bnk = accb[il]
                    MM(PB[bnk][:, acco[il]:acco[il] + W], pt[:, (i - i_lo) * P:(i - i_lo + 1) * P], VE[:, j, v0:v1],
                       (first[bnk] == k), (last[bnk] == k), [bpt, bVE[j]], [bPB[bnk]], skip=True)

            if kind != "B":
                for step in range(nU + 2):
                    if step < nU:
                        stage1(step)
                    if step >= 2:
                        stage3(step - 2)
                    tick()
            else:
                MSET("dve", Ls32[m], 0.0, [bLs32[m]])
                for step in range(nU + 2):
                    if step < nU:
                        stage1(step)
                    if 1 <= step <= nU:
                        stage2(step - 1)
                    if step >= 2:
                        stage3(step - 2)
                    tick()

            exhaust_gen(ev[0])
            ev[0] = None
            if kind == "A":
                CP("dve", accst[:, 0:258], PB[4][:, 0:258], [bPB[4]], [baccst])
                CP("dve", accst[:, 258:516], PB[5][:, 0:258], [bPB[5]], [baccst])
                accv = [accst[:, (il // 2) * 258 + (il % 2) * 129:(il // 2) * 258 + (il % 2) * 129 + 129] for il in range(4)]
            else:
                CP("dve", accst[:, 0:260], PB[accb[0]][:, 0:260], [bPB[accb[0]]], [baccst])
                accv = [accst[:, il * 65:(il + 1) * 65] for il in range(4)]

            def il_stages(il):
                i = 4 * c + il
                acc = accv[il]
                st = []
                if kind == "A":
                    rl, brl_ = rlrot.next()
                    if m == 0:
                        def a0():
                            RCP(rl, acc[:, 128:129], [baccst], [brl_])
                            TS(A0[:, il, :], acc[:, 0:128], rl, None, ALU.mult, None, [baccst, brl_], [bA0[il]])
                        st.append(a0)
                    else:
                        os_, bos_ = osrot.next()
                        s1_, bs1_ = ss1rot.next()

                        def a1():
                            RCP(rl, acc[:, 128:129], [baccst], [brl_])
                            TS(os_, acc[:, 0:128], rl, None, ALU.mult, None, [baccst, brl_], [bos_])
                            STT(os_, os_, neglam[:, 0:1], A0[:, il, :], ALU.mult, ALU.add, [bos_, bA0[il], b_lay], [bos_])

                        def a2():
                            AV(ob[:, il, :], os_, AF.Square, [bos_], [bs1_, bob], accum=s1_)

                        def a3():
                            AV(s1_, s1_, AF.Ln, [bs1_], [bs1_], scale=1.0 / 128, bias=EPS)
                            AV(s1_, s1_, AF.Exp, [bs1_], [bs1_], scale=-0.5)

                        def a4():
                            STT(ob[:, il, :], os_, s1_, gsub[:], ALU.mult, ALU.mult, [bos_, bs1_, b_lay], [bob])
                        st += [a1, a2, a3, a4]
                elif kind == "B":
                    st.append(lambda: CP("dve", ob[:, il, m * 64:(m + 1) * 64], acc[:, 0:64], [baccst], [bob]))
                elif kind == "C":
                    rl, brl_ = rlrot.next()

                    def c0():
                        RCP(rl, acc[:, 64:65], [baccst], [brl_])
                        TS(ob[:, il, m * 64:(m + 1) * 64], acc[:, 0:64], rl, None, ALU.mult, None, [baccst, brl_], [bob])
                    st.append(c0)
                else:
                    dst = accD[:, i, m * 65:(m + 1) * 65]
                    if idx == 0:
                        st.append(lambda: CP("dve", dst, acc, [baccst], [baccD[i]]))
                    else:
                        st.append(lambda: TT("dve", dst, acc, dst, ALU.add, [baccst, baccD[i]], [baccD[i]]))
                    if idx == 2:
                        rl, brl_ = rlrot.next()

                        def d1():
                            RCP(rl, accD[:, i, m * 65 + 64:m * 65 + 65], [baccD[i]], [brl_])
                            TS(ob[:, il, m * 64:(m + 1) * 64], accD[:, i, m * 65:m * 65 + 64], rl, None, ALU.mult, None,
                               [baccD[i], brl_], [bob])
                        st.append(d1)
                if m == 1 and (kind != "D" or idx == 2):
                    s_ = 4 + (rots["tr"] % 4)
                    rots["tr"] += 1
                    st.append(lambda: TRN(TRB[:, s_ * P:(s_ + 1) * P], ob[:, il, :], [bob, b_ident], [bTR[s_]]))
                    st.append(lambda: CP("dve", oT[:, chunk_o, i * P:(i + 1) * P], TRB[:, s_ * P:(s_ + 1) * P], [bTR[s_]],
                                         [boT[chunk_o][i]]))
                return st

            def evac_gen():
                lists = [il_stages(il) for il in range(4)]
                for pair in ((0, 1), (2, 3)):
                    k = 0
                    while True:
                        did = False
                        for il in pair:
                            if k < len(lists[il]):
                                lists[il][k]()
                                did = True
                        if not did:
                            break
                        k += 1
                        yield "e"

            return evac_gen()

        BR_CHUNKS = {0: [6, 7, 8, 9], 1: [0, 1, 2], 2: [3, 4, 5], 3: [10]}

        def merge(l):
            S.alias(lo_merge, lo_all)
            S.alias(bmgc[1], ex_all)
            S.alias(bmgc[0] + [bmacc], set1_bufs)
            for hf in range(2):
                for dc in range(8):
                    ds = rots["ds"] % 2
                    rots["ds"] += 1
                    bw = bWBh[2 * ds] + bWBh[2 * ds + 1]
                    gv = WB[:, 2 * ds:2 * ds + 2, :].rearrange("p a b -> p (a b)").rearrange("p (i k c) -> p i k c", i=4, k=KC)
                    for i in range(4):
                        k = load_chunk(l, cid_gate(dc, i))
                        cast_chunk(k, gv[:, i], lambda v: v.rearrange("p (k c) -> p k c", k=KC), bw)
                    wbd, bwbd_ = wbdrot.next()
                    k = load_chunk(l, cid_wbr(dc, 0))
                    cast_chunk(k, wbd[:, 0:8, :], lambda v: v.rearrange("p (f c) -> p f c", f=8), [bwbd_])
                    k = load_chunk(l, cid_wbr(dc, 1))
                    cast_chunk(k, wbd[:, 8:11, :], lambda v: v[:, 0:384].rearrange("p (f c) -> p f c", f=3), [bwbd_])
                    for tc in range(2):
                        tok0 = hf * 1024 + tc * 512
                        tls = list(range(tok0 // P, tok0 // P + 4))
                        msl = macc[:, tc * 512:(tc + 1) * 512]
                        for i in range(4):
                            gb_ = rots["g"] % 2
                            rots["g"] += 1
                            for kc in range(KC):
                                MM(PB[gb_][:, :], gv[:, i, kc, :], hT[:, kc, tok0:tok0 + 512], (kc == 0), (kc == KC - 1),
                                   bw + [bhT[t] for t in tls], [bPB[gb_]])
                            gs, bgs = gsbrot.next()
                            AV(gs, PB[gb_][:, :], AF.Sigmoid, [bPB[gb_], b_cst], [bgs], bias=GBT[:, l, i * 8 + dc:i * 8 + dc + 1])
                            pb_ = 2 + (rots["p"] % 2)
                            rots["p"] += 1
                            chs = BR_CHUNKS[i]
                            for q, fc in enumerate(chs):
                                MM(PB[pb_][:, :], wbd[:, fc, :], oT[:, fc, tok0:tok0 + 512], (q == 0), (q == len(chs) - 1),
                                   [bwbd_] + [boT[fc][t] for t in tls], [bPB[pb_]])
                            if i == 0:
                                TT("dve", msl, gs, PB[pb_][:, :], ALU.mult, [bgs, bPB[pb_]], [bmacc])
                            else:
                                tm, btm = tmprot.next()
                                TT("dve", tm, gs, PB[pb_][:, :], ALU.mult, [bgs, bPB[pb_]], [btm])
                                if i < 3:
                                    TT("dve", msl, msl, tm, ALU.add, [bmacc, btm], [bmacc])
                                else:
                                    TT("dve", mgT[hf][:, dc, tc * 512:(tc + 1) * 512], msl, tm, ALU.add, [bmacc, btm],
                                       [bmgc[hf][tc]])
            for hf in range(2):
                for t in range(hf * 8, hf * 8 + 8):
                    if t < 11:
                        S.alias([bX[t]], boT[t])
                    elif t < 15:
                        S.alias([bX[t]], bmgc[0])
                    else:
                        S.alias([bX[t]], [bmacc])
                    DMAX_(XR[:, t, :], xsc_d[t], [], [bX[t]], f"x{t}")
                for ch in range(2):
                    ds = rots["ds"] % 2
                    rots["ds"] += 1
                    bw = bWBh[2 * ds] + bWBh[2 * ds + 1]
                    wv = WB[:, 2 * ds:2 * ds + 2, :].rearrange("p a b -> p (a b)").rearrange("p (d c) -> p d c", d=8)
                    for q in range(4):
                        k = load_chunk(l, cid_wout(ch, q))
                        cast_chunk(k, wv[:, 2 * q:2 * q + 2, :], lambda v: v.rearrange("p (d c) -> p d c", d=2), bw)
                    for tt in range(8):
                        t = hf * 8 + tt
                        yb = 4 + (rots["y"] % 2)
                        rots["y"] += 1
                        for dc in range(8):
                            MM(PB[yb][:, :], mgT[hf][:, dc, tt * P:(tt + 1) * P], wv[:, dc, :], (dc == 0), (dc == 7),
                               [bmgc[hf][tt // 4]] + bw, [bPB[yb]])
                        TT("dve", XR[:, t, ch * 512:(ch + 1) * 512], PB[yb][:, :], XR[:, t, ch * 512:(ch + 1) * 512], ALU.add,
                           [bPB[yb], bX[t]], [bX[t]])

        def mixer(l):
            S.alias(mixer_ar, ar_all)
            S.alias(bhT[8:16], bxn + bsg)
            S.alias(bxn, ex_all)
            S.alias(bxn, ar_all)
            for t in range(NT):
                make_hT(t, t * P, l * 3 + 1, "mix")
            for t in range(NT):
                DMA(lambda e, t=t: e.dma_start(out=xsc_d[t], in_=XR[:, t, :]), r=[bX[t]], dq=f"x{t}")
            for cch in range(11):
                S.alias(boT[cch], [bX[cch]])
            S.alias(bmgc[0], bX[11:15])
            S.alias([bmacc], [bX[15]])
            layer_consts(l)
            S.alias(set1_bufs, bX[11:16])
            for k_ in range(2):
                MSET("pool", QTp_s[k_][0][64:128, :], 0.0, bQK_s[k_])
                MSET("pool", QTp_s[k_][1][0:64, :], 0.0, bQK_s[k_])
            S.alias(ex_B, ex_all)

            def exhaust(g):
                if g is not None:
                    for _ in g:
                        pass

            exhaust(inproj_gen(l, 0, HPUS[0][0], HPUS[0][1], 0))
            for n, (kind, idx) in enumerate(HPUS):
                if kind == "D" and idx == 0:
                    S.alias(baccD, ex_all)
                nxt = (inproj_gen(l, n + 1, HPUS[n + 1][0], HPUS[n + 1][1], (n + 1) % 2, need_barrier=(kind in ("B", "C")))
                       if n + 1 < len(HPUS) else None)
                cnt = [0]
                blocked = [False]
                if kind == "D":
                    jl = lambda c_: max(0, 4 * c_ - DMAX[idx])
                    tot = 2 * sum((4 * c_ + 4 - jl(c_)) + 2 for c_ in range(4))
                else:
                    tot = 96
                cad = 1

                def tick(nxt=nxt, cnt=cnt, blocked=blocked, cad=cad, spt=(2 if (kind in ("B", "D") or (n + 1 < len(HPUS) and HPUS[n + 1][0] == "C")) else 1)):
                    cnt[0] += 1
                    if nxt is None or blocked[0] or cnt[0] % cad != 0:
                        return
                    for _ in range(spt):
                        if next(nxt, "done") == "barrier":
                            blocked[0] = True
                            break

                attention(l, n, kind, idx, n % 2, tick)
                exhaust(nxt)
            if debug and l == 0:
                DMAX_(dbg_d, XRf[:, 0:11264], [b for row in boT for b in row], [], "dbg")
            merge(l)

        for b in range(nb):
            for t in range(NT):
                DMA(lambda e, b=b, t=t: e.dma_start(out=XR[:, t, :], in_=x_d[b, t * P:(t + 1) * P, :]),
                    w=[bX[t]], dq=f"x{t}")
            done = False
            for l in range(layers):
                ffn(l, 0)
                if stop == "ffn1":
                    done = True
                    break
                mixer(l)
                if stop == "mix":
                    done = True
                    break
                ffn(l, 1)
            for t in range(NT):
                DMA(lambda e, b=b, t=t: e.dma_start(out=out_d[b, t * P:(t + 1) * P, :], in_=XR[:, t, :]),
                    r=[bX[t]], dq=f"x{t}")
        S.emit(final_dqs=[f"x{t}" for t in range(NT)] + (["dbg"] if debug else []))
    return nc


def prep_shared(inp):
    L = DEPTH
    ws = np.zeros((L, NCH, P, 1024), np.float32)

    def kcm(w, c0, n=128):
        return w[:, c0:c0 + n].reshape(KC, P, n).transpose(1, 0, 2)

    for l in range(L):
        for j, (wi, wo) in enumerate(((inp["ffn1_w_in"], inp["ffn1_w_out"]), (inp["ffn2_w_in"], inp["ffn2_w_out"]))):
            for i in range(FC):
                for gu in range(2):
                    ws[l, cid_ffn(j, i, gu)] = kcm(wi[l], gu * FH + i * 128).reshape(P, 1024)
            for r in range(FC):
                ws[l, cid_ffo(j, r)] = wo[l, r * 128:(r + 1) * 128, :]
        win = inp["w_in"][l]
        for n, (kind, idx) in enumerate(HPUS):
            for part, c0 in enumerate(hpu_cols(kind, idx)):
                ws[l, cid_inp(n, part)] = kcm(win, c0).reshape(P, 1024)
        for dc in range(8):
            for i in range(4):
                ws[l, cid_gate(dc, i)] = kcm(win, OFF_GL + i * 1024 + dc * 128).reshape(P, 1024)
        wbr = inp["w_branch"][l]
        rows = []
        for (kind, idx) in [("B", 0), ("B", 1), ("B", 2), ("C", 0), ("C", 1), ("C", 2), ("A", 0), ("A", 1), ("A", 2), ("A", 3), ("D", 0)]:
            r0 = {"A": 0, "B": 512, "C": 896, "D": 1280}[kind] + idx * 128
            rows.append(wbr[r0:r0 + 128])
        wbr_o = np.stack(rows)
        for dc in range(8):
            blk = wbr_o[:, :, dc * 128:(dc + 1) * 128].transpose(1, 0, 2)
            ws[l, cid_wbr(dc, 0)] = blk[:, 0:8].reshape(P, 1024)
            ws[l, cid_wbr(dc, 1), :, 0:384] = blk[:, 8:11].reshape(P, 384)
        wo_ = inp["w_out"][l]
        for half in range(2):
            for q in range(4):
                blk = wo_[q * 256:(q + 1) * 256, half * 512:(half + 1) * 512].reshape(2, P, 512).transpose(1, 0, 2)
                ws[l, cid_wout(half, q)] = blk.reshape(P, 1024)
    gT = np.stack([np.stack([inp[k][l].reshape(KC, P).T for k in ("ffn1_norm", "mix_norm", "ffn2_norm")]) for l in range(L)])
    gb = np.stack([inp["gate_bias"][l].reshape(4, 8, P).transpose(2, 0, 1).reshape(P, 32) for l in range(L)])
    pv = np.zeros((L, NPV), np.float32)
    for l in range(L):
        pv[l, PV_FB:PV_FB + 6] = inp["forget_bias"][l]
        pv[l, PV_AQ:PV_AQ + 64] = inp["a_q_norm"][l]
        pv[l, PV_AK:PV_AK + 64] = inp["a_k_norm"][l]
        pv[l, PV_LAM:PV_LAM + 256] = inp["a_lambda"][l].reshape(-1)
        pv[l, PV_SUB:PV_SUB + 128] = inp["a_subln"][l]
        pv[l, PV_CQ:PV_CQ + 64] = inp["c_q_norm"][l]
        pv[l, PV_CK:PV_CK + 64] = inp["c_k_norm"][l]
        pv[l, PV_DQ:PV_DQ + 64] = inp["d_q_norm"][l]
        pv[l, PV_DK:PV_DK + 64] = inp["d_k_norm"][l]
        pv[l, PV_REL31:PV_REL31 + 10] = inp["rel_table"][31]
    wcf = np.stack([inp["w_in"][l][:, OFF_CF:OFF_CF + 6].reshape(KC, P, 6).transpose(1, 0, 2) for l in range(L)])
    rel = inp["rel_table"]
    kk = np.arange(P)[:, None]
    qq = np.arange(P)[None, :]
    biasA = np.zeros((P, 8, P), np.float32)
    for h in range(4):
        for dl in range(2):
            biasA[:, h * 2 + dl, :] = rel[rel_bucket_np(np.maximum(128 * dl + qq - kk, 0)), h]
    biasD = np.zeros((P, 18, P), np.float32)
    masks = np.zeros((P, NMASK, P), np.float32)
    for ti, (g, ty) in enumerate(D_TYPES):
        window, dil = D_PAIRS[g]
        dist = 128 * d_type_dist(g, ty) + qq - kk
        valid = (dist >= 0) & (dist <= window) & (dist % dil == 0)
        if ty == "far":
            valid = (dist % dil == 0)
        masks[:, ti, :] = np.where(valid, 0.0, NEG)
        for hh in range(2):
            biasD[:, ti * 2 + hh, :] = rel[rel_bucket_np(np.maximum(dist, 0)), 4 + 2 * g + hh]
    masks[:, M_STRICT, :] = np.where(kk < qq, 0.0, NEG)
    csel = np.zeros((P, 8), np.float32)
    csel[0, 0] = -1.0
    csel[64, 1] = -1.0
    csel[32, 2] = 1.0
    csel[96, 2] = 1.0
    csel[32, 3] = 1.0
    csel[96, 4] = 1.0
    csel[0, 5] = 1.0
    csel[64, 5] = 1.0
    return {"wstream": ws, "gT": np.ascontiguousarray(gT), "gbias": np.ascontiguousarray(gb),
            "pvec": pv.reshape(1, -1), "wcf": np.ascontiguousarray(wcf), "biasA": biasA, "biasD": biasD,
            "masks": masks, "csel": csel}


def kernel(**inputs):
    inp = {k: np.asarray(v, dtype=np.float32) for k, v in inputs.items()}
    shared = prep_shared(inp)
    n = 8
    nb = inp["x"].shape[0] // n
    nc = build_program(nb=nb, layers=DEPTH)
    in_maps = []
    for c in range(n):
        m = dict(shared)
        m["x"] = np.ascontiguousarray(inp["x"][c * nb:(c + 1) * nb])
        in_maps.append(m)
    res = run_bass_kernel_spmd(nc, in_maps, core_ids=list(range(n)))
    return np.concatenate([r["out"] for r in res.results], axis=0).astype(np.float32)
```
